# Optimizing a Trainium2 kernel written in Bass

```python
import jax, jax.numpy as jnp
from jax import lax
import numpy as np

D_MODEL = 2048
BATCH = 2
SEQ = 4096
DEPTH = 2

N_MIXERS = 2
EPS = 1e-6
D_FF = 5632
GM_WIDTH = D_MODEL
CHUNK = 128
GM_GROUPS = 16
GM_GROUP_DIM = GM_WIDTH // GM_GROUPS
CONV_WIDTH = D_MODEL
CONV_K = 31
N_SUB = 3
N_MOD = 3
N_A = (DEPTH + 1) // 2
N_B = DEPTH // 2

kernel_name = "macaron_gmlp_conformer_hybrid"


def rms_norm(x, g):
    xf = x.astype(jnp.float32)
    y = xf * lax.rsqrt(jnp.mean(xf * xf, axis=-1, keepdims=True) + EPS)
    return (y * g.astype(jnp.float32)).astype(x.dtype)


def layer_norm(x, g, b):
    xf = x.astype(jnp.float32)
    mu = jnp.mean(xf, axis=-1, keepdims=True)
    var = jnp.mean(jnp.square(xf - mu), axis=-1, keepdims=True)
    y = (xf - mu) * lax.rsqrt(var + EPS)
    return (y * g.astype(jnp.float32) + b.astype(jnp.float32)).astype(x.dtype)


def modulate(h, shift, scale):
    return h * (1 + scale[:, None, :]) + shift[:, None, :]


def swiglu_ffn(h, w_in, w_out):
    gate, up = jnp.split(h @ w_in, 2, axis=-1)
    return (jax.nn.silu(gate) * up) @ w_out


def gmlp_mixer(h, w_in, ln_g, ln_b, ws, bs, w_out):
    b, t, _ = h.shape
    z = jax.nn.gelu(h @ w_in, approximate=False)
    u, v = jnp.split(z, 2, axis=-1)
    v = layer_norm(v, ln_g, ln_b)
    v = v.reshape(b, t // CHUNK, CHUNK, GM_GROUPS, GM_GROUP_DIM)
    causal = jnp.tril(jnp.ones((CHUNK, CHUNK), dtype=bool))
    ws_c = jnp.where(causal[None], ws, jnp.zeros_like(ws))
    v = jnp.einsum("hts,bcshd->bcthd", ws_c, v) + bs.T[None, None, :, :, None]
    s = u * v.reshape(b, t, GM_WIDTH)
    return s @ w_out


def conv_mixer(h, w_in, b_in, dw_w, dw_b, ln_g, ln_b, w_out, b_out):
    a, g = jnp.split(h @ w_in + b_in, 2, axis=-1)
    y = a * jax.nn.sigmoid(g)
    y = lax.conv_general_dilated(
        y, dw_w[:, None, :].astype(y.dtype),
        window_strides=(1,), padding=[(CONV_K - 1, 0)],
        dimension_numbers=("NWC", "WIO", "NWC"),
        feature_group_count=CONV_WIDTH) + dw_b
    y = jax.nn.silu(layer_norm(y, ln_g, ln_b))
    return y @ w_out + b_out


def setup_inputs(seed: int = 0) -> dict:
    key = jax.random.key(seed)
    ks = jax.random.split(key, 24)
    D, F, E, C, H, L = D_MODEL, D_FF, GM_WIDTH, CONV_WIDTH, GM_GROUPS, CHUNK
    nrm = lambda k, shape, s: (jax.random.normal(k, shape, jnp.float32) * s).astype(jnp.float32)

    x = nrm(ks[0], (BATCH, SEQ, D), 1.0)
    c = nrm(ks[1], (BATCH, D), 1.0)
    ada_w = nrm(ks[2], (DEPTH, D, N_SUB * N_MOD * D), 0.1 * D ** -0.5)
    ada_b = nrm(ks[3], (DEPTH, N_SUB, N_MOD, D), 0.02).at[:, :, 2].add(1.0).reshape(DEPTH, N_SUB * N_MOD * D)
    norm_g = 1.0 + nrm(ks[4], (DEPTH, N_SUB, D), 0.02)
    ffn_w_in = nrm(ks[5], (DEPTH, 2, D, 2 * F), D ** -0.5)
    ffn_w_out = nrm(ks[6], (DEPTH, 2, F, D), F ** -0.5)

    gm_w_in = nrm(ks[7], (N_A, D, 2 * E), D ** -0.5)
    gm_ln_g = 1.0 + nrm(ks[8], (N_A, E), 0.02)
    gm_ln_b = nrm(ks[9], (N_A, E), 0.02)
    gm_ws = nrm(ks[10], (N_A, H, L, L), L ** -0.5)
    gm_bs = 1.0 + nrm(ks[11], (N_A, H, L), 0.02)
    gm_w_out = nrm(ks[12], (N_A, E, D), E ** -0.5)

    cv_w_in = nrm(ks[13], (N_B, D, 2 * C), D ** -0.5)
    cv_b_in = nrm(ks[14], (N_B, 2 * C), 0.02)
    cv_dw_w = nrm(ks[15], (N_B, CONV_K, C), CONV_K ** -0.5)
    cv_dw_b = nrm(ks[16], (N_B, C), 0.02)
    cv_ln_g = 1.0 + nrm(ks[17], (N_B, C), 0.02)
    cv_ln_b = nrm(ks[18], (N_B, C), 0.02)
    cv_w_out = nrm(ks[19], (N_B, C, D), C ** -0.5)
    cv_b_out = nrm(ks[20], (N_B, D), 0.02)

    final_g = 1.0 + nrm(ks[21], (D,), 0.02)
    return {"x": x, "c": c, "ada_w": ada_w, "ada_b": ada_b, "norm_g": norm_g,
            "ffn_w_in": ffn_w_in, "ffn_w_out": ffn_w_out,
            "gm_w_in": gm_w_in, "gm_ln_g": gm_ln_g, "gm_ln_b": gm_ln_b,
            "gm_ws": gm_ws, "gm_bs": gm_bs, "gm_w_out": gm_w_out,
            "cv_w_in": cv_w_in, "cv_b_in": cv_b_in, "cv_dw_w": cv_dw_w, "cv_dw_b": cv_dw_b,
            "cv_ln_g": cv_ln_g, "cv_ln_b": cv_ln_b, "cv_w_out": cv_w_out, "cv_b_out": cv_b_out,
            "final_g": final_g}


def reference(x, c, ada_w, ada_b, norm_g, ffn_w_in, ffn_w_out,
              gm_w_in, gm_ln_g, gm_ln_b, gm_ws, gm_bs, gm_w_out,
              cv_w_in, cv_b_in, cv_dw_w, cv_dw_b, cv_ln_g, cv_ln_b, cv_w_out, cv_b_out,
              final_g):
    bsz = x.shape[0]
    cond = jax.nn.silu(c)
    for i in range(DEPTH):
        mod = (cond @ ada_w[i] + ada_b[i]).reshape(bsz, N_SUB, N_MOD, D_MODEL)
        shift, scale, gate = mod[:, :, 0], mod[:, :, 1], mod[:, :, 2]

        h = modulate(rms_norm(x, norm_g[i, 0]), shift[:, 0], scale[:, 0])
        x = x + 0.5 * gate[:, 0, None, :] * swiglu_ffn(h, ffn_w_in[i, 0], ffn_w_out[i, 0])

        h = modulate(rms_norm(x, norm_g[i, 1]), shift[:, 1], scale[:, 1])
        j = i // N_MIXERS
        if i % N_MIXERS == 0:
            y = gmlp_mixer(h, gm_w_in[j], gm_ln_g[j], gm_ln_b[j], gm_ws[j], gm_bs[j], gm_w_out[j])
        else:
            y = conv_mixer(h, cv_w_in[j], cv_b_in[j], cv_dw_w[j], cv_dw_b[j],
                           cv_ln_g[j], cv_ln_b[j], cv_w_out[j], cv_b_out[j])
        x = x + gate[:, 1, None, :] * y

        h = modulate(rms_norm(x, norm_g[i, 2]), shift[:, 2], scale[:, 2])
        x = x + 0.5 * gate[:, 2, None, :] * swiglu_ffn(h, ffn_w_in[i, 1], ffn_w_out[i, 1])
    return rms_norm(x, final_g)
```

```python
import os
import numpy as np
import concourse.bass as bass
import concourse.mybir as mybir
from concourse.bass_utils import run_bass_kernel_spmd

F32 = mybir.dt.float32
BF16 = mybir.dt.bfloat16
AF = mybir.ActivationFunctionType
ALU = mybir.AluOpType
AX = mybir.AxisListType

D = 2048
FF = 5632
NFC = 44
T = 1152
TM = 1024
NCORE = 8
EPS = 1e-6
KC = 16
NS = 4
ADA_STEPS = 2
PIECE = 4096
TILES = [(0, 128), (128, 512), (640, 512)]
ALL_SUBS = [(0, 0), (0, 1), (0, 2), (1, 0), (1, 1), (1, 2)]

SM = {}
_o = 0
for _n, _w in [("c", 16), ("adab", 288), ("ng", 96), ("fg", 16), ("cvbin", 32), ("dw", 496), ("dwb", 16),
               ("cvlng", 16), ("cvlnb", 16), ("cvbout", 16), ("gmlng", 16), ("gmlnb", 16), ("hmask", 1), ("ident", 128)]:
    SM[_n] = _o
    _o += _w
NSM = _o
NGC = 2048 + 128 + 2048


def plan_pieces(subs, halo):
    P = []

    def ada(l, s):
        for cb in range(3):
            for kp in range(8):
                P.append(("ada", l, s, cb, kp))

    def ffn(l, s, nxt):
        f = s // 2
        for g in range(11):
            for j in range(4):
                P.append(("fin", l, f, 4 * g + j))
            if g == 0 and nxt is not None:
                ada(*nxt)
            if g >= 1:
                for hf in range(2):
                    P.append(("fout", l, f, g - 1, hf))
        for hf in range(2):
            P.append(("fout", l, f, 10, hf))

    def gm(nxt):
        first = True
        for (t0, n) in TILES:
            nch = n // 128
            c = 0
            while c < nch:
                for nt in range(4):
                    P.append(("gv", nt, 0))
                    P.append(("gv", nt, 1))
                c += 2
            for ep in range(8):
                P.append(("gu", ep))
            if first and nxt is not None:
                ada(*nxt)
            first = False
            for dp in range(8):
                P.append(("go", dp))

    def cv(nxt):
        first = True
        for ti, (t0, n) in enumerate(TILES):
            for cc in range(16):
                P.append(("ci", cc))
            if first and nxt is not None:
                ada(*nxt)
            first = False
            if ti == 0:
                continue
            for dp in range(8):
                P.append(("co", dp))

    ada(*subs[0])
    for i, (l, s) in enumerate(subs):
        nxt = subs[i + 1] if i + 1 < len(subs) else None
        if s != 1:
            ffn(l, s, nxt)
        elif l == 0:
            gm(nxt)
        else:
            cv(nxt)
    return P


def pack_piece(spec, inp):
    k = spec[0]

    def two_block(W, ca, cb_):
        a = W[:, ca:ca + 128].reshape(16, 128, 128)
        b = W[:, cb_:cb_ + 128].reshape(16, 128, 128)
        return np.concatenate([a, b], axis=2).transpose(1, 0, 2).reshape(128, PIECE)

    if k == "ada":
        _, l, s, cbk, kq = spec
        W = inp["ada_w"][l]
        c0 = s * 6144 + cbk * 1024
        blk = W[kq * 512:(kq + 1) * 512, c0:c0 + 1024].reshape(4, 128, 1024)
        return blk.transpose(1, 0, 2).reshape(128, PIECE)
    if k == "fin":
        _, l, f, fc = spec
        return two_block(inp["ffn_w_in"][l, f], fc * 128, FF + fc * 128)
    if k == "fout":
        _, l, f, g, hf = spec
        W = inp["ffn_w_out"][l, f]
        blk = W[g * 512:(g + 1) * 512, hf * 1024:(hf + 1) * 1024].reshape(4, 128, 1024)
        return blk.transpose(1, 0, 2).reshape(128, PIECE)
    if k == "gv":
        _, nt, kh = spec
        W = inp["gm_w_in"][0]
        blk = W[kh * 1024:(kh + 1) * 1024, 2048 + nt * 512:2048 + (nt + 1) * 512].reshape(8, 128, 512)
        return blk.transpose(1, 0, 2).reshape(128, PIECE)
    if k == "gv2":
        _, nt = spec
        W = inp["gm_w_in"][0]
        blk = W[:, 2048 + nt * 256:2048 + (nt + 1) * 256].reshape(16, 128, 256)
        return blk.transpose(1, 0, 2).reshape(128, PIECE)
    if k == "gu":
        _, ep = spec
        W = inp["gm_w_in"][0]
        blk = W[:, ep * 256:(ep + 1) * 256].reshape(16, 128, 2, 128)
        return blk.transpose(1, 2, 0, 3).reshape(128, PIECE)
    if k == "go":
        _, dp = spec
        return two_block(inp["gm_w_out"][0], dp * 256, dp * 256 + 128)
    if k == "ci":
        _, cc = spec
        return two_block(inp["cv_w_in"][0], cc * 128, 2048 + cc * 128)
    if k == "co":
        _, dp = spec
        return two_block(inp["cv_w_out"][0], dp * 256, dp * 256 + 128)
    raise ValueError(spec)


class Res:
    __slots__ = ("name", "w", "r")

    def __init__(self, name):
        self.name = name
        self.w = None
        self.r = []


class Gen:
    ENG = ("pe", "act", "dve", "pool", "sp")

    def __init__(self):
        self.ops = {e: [] for e in self.ENG}
        self.count = {e: 0 for e in self.ENG}
        self.waited = {e: {} for e in self.ENG}
        self.dma_count = {}
        self.pending = {}

    def _deps(self, eng, reads, writes, extra, skip_same=True):
        need = {}

        def add(ev, same_ok):
            if ev is None:
                return
            k, v = ev
            if k == eng and not same_ok and skip_same:
                return
            if need.get(k, 0) < v:
                need[k] = v
        for r in reads:
            add(r.w, True)
        for w in writes:
            add(w.w, True)
            for ev in w.r:
                add(ev, False)
        for ev in extra:
            add(ev, True)
        for ev in self.pending.pop(eng, ()):
            add(ev, True)
        waits = []
        wd = self.waited[eng]
        for k, v in need.items():
            if wd.get(k, 0) >= v:
                continue
            wd[k] = v
            waits.append((k, v))
        return waits

    def op(self, eng, reads, writes, fn, extra=()):
        waits = self._deps(eng, reads, writes, extra)
        self.count[eng] += 1
        ev = (eng, self.count[eng])
        self.ops[eng].append((waits, fn, (eng, self.count[eng])))
        for r in reads:
            r.r.append(ev)
        for w in writes:
            w.w = ev
            w.r = []
        return ev

    def dma(self, queue, semname, reads, writes, fn, extra=()):
        waits = self._deps(queue, reads, writes, extra, skip_same=False)
        self.dma_count[semname] = self.dma_count.get(semname, 0) + 16
        ev = (semname, self.dma_count[semname])
        self.ops[queue].append((waits, fn, (semname, 16)))
        for r in reads:
            r.r.append(ev)
        for w in writes:
            w.w = ev
            w.r = []
        return ev

    def final_wait(self, eng, evs):
        waits = self._deps(eng, [], [], evs)
        self.ops[eng].append((waits, None, None))


def build_program(subs, do_final=True):
    _, rec = _build(subs, do_final, None)
    nc, _ = _build(subs, do_final, rec)
    uniq = []
    for sp_ in rec:
        if sp_ not in uniq:
            uniq.append(sp_)
    return nc, uniq


def _build(subs, do_final, plan):
    nc = bass.Bass("TRN2", target_bir_lowering=False)
    rec = []
    uniq = {}
    uid = []
    for sp_ in (plan or []):
        if sp_ not in uniq:
            uniq[sp_] = len(uniq)
        uid.append(uniq[sp_])
    NU = max(1, len(uniq))
    NP = len(plan) if plan is not None else 10 ** 9

    xT_d = nc.dram_tensor("xT", [D, T], F32, kind="ExternalInput").ap()
    sm_d = nc.dram_tensor("smalls", [128, NSM], F32, kind="ExternalInput").ap()
    gc_d = nc.dram_tensor("gconst", [128, NGC], F32, kind="ExternalInput").ap()
    ws_d = nc.dram_tensor("wstream", [NU, 128, PIECE], F32, kind="ExternalInput").ap()
    out_d = nc.dram_tensor("outT", [D, TM], F32, kind="ExternalOutput").ap()

    NB = 106400
    SB = nc.alloc_sbuf_tensor("SB", [128, NB], BF16)
    cur = [0]

    def carve(units, dt=BF16, at=None):
        if at is None:
            at = cur[0]
            cur[0] += (units + 15) // 16 * 16
            assert cur[0] <= NB, ("SBUF overflow", cur[0])
        ap = SB[:, at:at + units]
        if dt == F32:
            ap = ap.bitcast(F32)
        return ap

    xs = carve(KC * T * 2, F32).rearrange("p (c t) -> p c t", c=KC)
    ring = [carve(PIECE) for _ in range(NS)]
    smalls = carve(NSM * 2, F32)
    rstd = carve(512 * 2, F32)
    tmpA = [carve(1024, F32) for _ in range(2)]
    accA = carve(2048, F32)
    condf = carve(32, F32)
    identb = carve(128)
    mods = carve(6 * 96 * 2, F32)
    ones1 = carve(128)
    onesD = carve(128)
    onef = carve(2, F32)
    epsc = carve(2, F32)
    condb = carve(16)
    smstat = carve(64 * 2, F32)
    LOC0 = cur[0]
    LOCN = NB - LOC0

    def loc_alloc():
        st = [LOC0]

        def f(units, dt=BF16):
            at = st[0]
            st[0] += (units + 15) // 16 * 16
            assert st[0] <= NB, ("LOC overflow", st[0] - LOC0, LOCN)
            return carve(units, dt, at=at)
        return f

    la = loc_alloc()
    hF = la(KC * T).rearrange("p (c t) -> p c t", c=KC)
    actb = [la(4 * T).rearrange("p (j t) -> p j t", j=4) for _ in range(2)]
    sgt = [la(1024, F32) for _ in range(2)]
    la = loc_alloc()
    hG = la(KC * 640).rearrange("p (c t) -> p c t", c=KC)
    vt_off = LOC0 + KC * 640
    vt = [la(4096, F32) for _ in range(2)]
    vn = [la(2048) for _ in range(2)]
    junk = la(2048)
    Gs = la(KC * 640).rearrange("p (c t) -> p c t", c=KC)
    Qc = la(2048 * 2, F32).rearrange("p (h t) -> p h t", h=16)
    wsTm = la(2048).rearrange("p (h t) -> p h t", h=16)
    utmp = [la(1024, F32) for _ in range(2)]
    gtmp = carve(NGC * 2, F32, at=vt_off)
    assert NGC * 2 <= 2 * 4096 + 2 * 2048 + 2048
    la = loc_alloc()
    hC = la(KC * 640).rearrange("p (c t) -> p c t", c=KC)
    cact = hC
    c2 = la(KC * 512 * 2, F32).rearrange("p (c t) -> p c t", c=KC)
    ybuf = [la(544) for _ in range(2)]
    ytail = la(KC * 30).rearrange("p (c t) -> p c t", c=KC)
    stmp = [la(1024, F32) for _ in range(2)]
    dg = [la(31 * 128).rearrange("p (j c) -> p j c", j=31) for _ in range(2)]
    cbq = [la(512) for _ in range(2)]
    s1s = la(1024, F32)
    rstdc = la(1024, F32)

    banks = [nc.alloc_psum_tensor("bank%d" % i, [128, 512], F32) for i in range(8)]
    zb = banks[0:4]
    yb = banks[4:6]
    sbk = banks[6]
    mbk = banks[7]

    G = Gen()
    R = lambda n: Res(n)
    r_x = [[R("x") for _ in range(9)] for _ in range(KC)]
    r_h = [[R("h") for _ in range(9)] for _ in range(KC)]
    r_ring = [R("ring%d" % i) for i in range(NS)]
    r_zb = [R("zb") for _ in range(4)]
    r_yb = [R("yb") for _ in range(2)]
    r_sb = R("sb")
    r_mb = R("mb")
    r_sm = R("smalls")
    r_rstd = [R("rstd") for _ in range(9)]
    r_tmpA = [R("tmpA") for _ in range(2)]
    r_acc = R("accA")
    r_mods = [R("mods") for _ in range(6)]
    r_const = R("const")
    r_loc = R("loc")
    r_act = [[[R("act") for _ in range(3)] for _ in range(4)] for _ in range(2)]
    r_sgt = [R("sgt") for _ in range(2)]

    def gran(t0, n):
        return list(range(t0 // 128, (t0 + n) // 128))

    ringst = {"next_dma": 0, "released": 0, "acq": 0}

    def pump():
        while ringst["next_dma"] < NP and ringst["next_dma"] - NS < ringst["released"]:
            j = ringst["next_dma"]
            slot = j % NS
            src = ws_d[uid[j] if plan is not None else 0]
            dst = ring[slot]
            G.dma("pool", "ring%d" % slot, [], [r_ring[slot]],
                  lambda e, dst=dst, src=src: e.dma_start(out=dst, in_=src))
            ringst["next_dma"] += 1

    held = []
    relflag = {}

    def acquire(spec, owner="main"):
        i = ringst["acq"]
        rec.append(spec)
        if plan is not None:
            assert plan[i] == spec, (i, plan[i], spec)
        assert i < ringst["next_dma"], "ring deadlock: piece %d not prefetchable" % i
        ringst["acq"] += 1
        held.append((i, owner))
        slot = i % NS
        return ring[slot], r_ring[slot]

    def release(k=1, owner="main"):
        for _ in range(k):
            for hi, (i, ow) in enumerate(held):
                if ow == owner:
                    held.pop(hi)
                    relflag[i] = True
                    break
            else:
                raise AssertionError("release without held piece for " + owner)
        while relflag.get(ringst["released"], False):
            del relflag[ringst["released"]]
            ringst["released"] += 1
        pump()

    G.dma("sp", "lds", [], [r_sm], lambda e: e.dma_start(out=smalls, in_=sm_d))
    xv = xT_d.rearrange("(c p) t -> p c t", p=128)
    pump()
    xdelay = [r_ring[i].w for i in range(NS) if r_ring[i].w is not None]
    for q in range(4):
        wr = [r_x[kc][g] for kc in range(4 * q, 4 * q + 4) for g in range(9)]
        G.dma("sp", "ld", [], wr,
              lambda e, q=q: e.dma_start(out=xs[:, 4 * q:4 * q + 4, :], in_=xv[:, 4 * q:4 * q + 4, :]))
    tot = ("ld", G.dma_count["ld"])
    for kc in range(KC):
        for g in range(9):
            r_x[kc][g].w = tot
    pump()

    def sm(name, a, b):
        o = SM[name]
        return smalls[:, o + a:o + b]

    G.op("dve", [], [r_const], lambda e: e.memset(ones1, 1.0))
    G.op("dve", [], [r_const], lambda e: e.memset(onesD, 1.0 / D))
    G.op("dve", [], [r_const], lambda e: e.memset(onef, 1.0))
    G.op("dve", [], [r_const], lambda e: e.memset(epsc, EPS))
    G.op("dve", [r_sm], [r_const], lambda e: e.tensor_copy(out=identb, in_=sm("ident", 0, 128)))
    G.op("act", [r_sm], [r_const], lambda e: e.activation(out=condf, in_=sm("c", 0, 16), func=AF.Silu))

    kz = [0]
    ky3 = [0]
    y3 = [(yb[0], r_yb[0]), (yb[1], r_yb[1]), (mbk, r_mb)]
    ky = [0]
    kt = [0]

    def modv(idx, what):
        base = idx * 96
        o = {"shift": 0, "scale": 16, "gate": 32, "A": 48, "HG": 64, "BG": 80}[what]
        return mods[:, base + o:base + o + 16]

    def ada_gen(l, s, eng, part="all"):
        idx = l * 3 + s
        base = idx * 96
        cbks = {"all": range(6), "ss": range(4), "gate": range(4, 6)}[part]
        for cbk in cbks:
            for kq in range(4):
                pc, rp = acquire(("ada", l, s, cbk, kq), owner="ada")
                for kk in range(4):
                    kc = 4 * kq + kk
                    if kc == 0:
                        G.op(eng, [rp, r_const], [r_acc],
                             lambda e, pc=pc: e.tensor_scalar(out=accA, in0=pc[:, 0:1024], scalar1=condf[:, 0:1],
                                                              scalar2=None, op0=ALU.mult))
                    else:
                        G.op(eng, [rp, r_const, r_acc], [r_acc],
                             lambda e, pc=pc, kk=kk, kc=kc: e.scalar_tensor_tensor(
                                 out=accA, in0=pc[:, kk * 1024:(kk + 1) * 1024], scalar=condf[:, kc:kc + 1], in1=accA,
                                 op0=ALU.mult, op1=ALU.add))
                    if kk == 3:
                        release(1, owner="ada")
                    yield False
            yi = ky[0] % 2
            ky[0] += 1

            def f2(e, yi=yi):
                ins = None
                for jj in range(8):
                    ins = e.matmul(yb[yi][:, jj:jj + 1], lhsT=accA[:, jj * 128:(jj + 1) * 128], rhs=onef[:, 0:1],
                                   start=True, stop=True)
                return ins
            G.op("pe", [r_acc, r_const], [r_yb[yi]], f2)
            G.op("dve", [r_yb[yi], r_sm], [r_mods[idx]],
                 lambda e, yi=yi, cbk=cbk: e.tensor_tensor(out=mods[:, base + cbk * 8:base + cbk * 8 + 8], in0=yb[yi][:, 0:8],
                                                           in1=sm("adab", idx * 48 + cbk * 8, idx * 48 + cbk * 8 + 8), op=ALU.add))
        if part in ("all", "ss"):
            G.op("dve", [r_mods[idx], r_sm], [r_mods[idx]],
                 lambda e: e.scalar_tensor_tensor(out=modv(idx, "A"), in0=modv(idx, "scale"), scalar=1.0,
                                                  in1=sm("ng", idx * 16, idx * 16 + 16), op0=ALU.add, op1=ALU.mult))
        if part == "ss":
            yield True
            return
        gs = 1.0 if s == 1 else 0.5
        G.op("dve", [r_mods[idx]], [r_mods[idx]],
             lambda e: e.tensor_scalar(out=modv(idx, "HG"), in0=modv(idx, "gate"), scalar1=gs, scalar2=None,
                                       op0=ALU.mult))
        if (l, s) == (1, 1):
            G.op("dve", [r_mods[idx], r_sm], [r_mods[idx]],
                 lambda e: e.tensor_tensor(out=modv(idx, "BG"), in0=modv(idx, "gate"),
                                           in1=sm("cvbout", 0, 16), op=ALU.mult))
        yield True

    adajob = {"gen": None, "tick": 0}

    def ada_start(jobs, eng="dve"):
        def chain():
            for job in jobs:
                l_, s_ = job[0], job[1]
                part = job[2] if len(job) > 2 else "all"
                for _ in ada_gen(l_, s_, eng, part):
                    yield False
                if part == "gate":
                    adajob["gate_done"] = True
            yield True
        adajob["gen"] = chain()
        adajob["tick"] = 0
        adajob["steps"] = ADA_STEPS * max(1, len(jobs))
        adajob["njobs"] = min(2, max(1, len(jobs)))

    def ada_drain():
        g = adajob["gen"]
        if g is not None:
            adajob["gen"] = None
            for _ in g:
                pass

    def ada_fine():
        g = adajob["gen"]
        if g is None:
            return
        adajob["gen"] = None
        done = next(g)
        adajob["gen"] = None if done else g

    def mrelease(k=1):
        release(k)
        g = adajob["gen"]
        if g is None or adajob.get("fine"):
            return
        adajob["tick"] += 1
        adajob["gen"] = None
        done = False
        for _ in range(adajob["steps"]):
            done = next(g)
            if done:
                break
        adajob["gen"] = None if done else g

    def norm_tile(idx, t0, n, hbuf, hoff, final=False):
        gr = gran(t0, n)
        hgr = gran(hoff, n)
        for kc in range(KC):
            G.op("act", [r_x[kc][g] for g in gr], [r_h[kc][g] for g in hgr],
                 lambda e, kc=kc: e.activation(out=hbuf[:, kc, hoff:hoff + n], in_=xs[:, kc, t0:t0 + n],
                                               func=AF.Square))

            def f(e, kc=kc):
                return e.matmul(sbk[:, 0:n], lhsT=onesD, rhs=hbuf[:, kc, hoff:hoff + n],
                                start=(kc == 0), stop=(kc == KC - 1))
            G.op("pe", [r_h[kc][g] for g in hgr] + [r_const], [r_sb], f)
        G.op("act", [r_sb], [r_rstd[0]],
             lambda e: e.activation(out=rstd[:, 0:n], in_=sbk[:, 0:n], func=AF.Sqrt, bias=epsc[:, 0:1], scale=1.0))
        G.op("dve", [r_rstd[0]], [r_rstd[0]],
             lambda e: e.reciprocal(out=rstd[:, 0:n], in_=rstd[:, 0:n]))
        for kc in range(KC):
            if final:
                G.op("dve", [r_x[kc][g] for g in gr] + [r_rstd[0]] + [r_sm],
                     [r_x[kc][g] for g in gr],
                     lambda e, kc=kc: e.scalar_tensor_tensor(out=xs[:, kc, t0:t0 + n], in0=xs[:, kc, t0:t0 + n],
                                                             scalar=sm("fg", kc, kc + 1), in1=rstd[:, 0:n],
                                                             op0=ALU.mult, op1=ALU.mult))
                continue
            k = kt[0] % 2
            kt[0] += 1
            G.op("dve", [r_x[kc][g] for g in gr] + [r_rstd[0]] + [r_mods[idx]], [r_tmpA[k]],
                 lambda e, kc=kc, k=k: e.scalar_tensor_tensor(out=tmpA[k][:, 0:n], in0=xs[:, kc, t0:t0 + n],
                                                              scalar=modv(idx, "A")[:, kc:kc + 1],
                                                              in1=rstd[:, 0:n], op0=ALU.mult, op1=ALU.mult))
            G.op("act", [r_tmpA[k], r_mods[idx]], [r_h[kc][g] for g in hgr],
                 lambda e, kc=kc, k=k: e.activation(out=hbuf[:, kc, hoff:hoff + n], in_=tmpA[k][:, 0:n],
                                                    func=AF.Identity, bias=modv(idx, "shift")[:, kc:kc + 1],
                                                    scale=1.0))

    def x_update(idx, dc, t0, n, ybank, r_ybank, extra_bias=False):
        gr = gran(t0, n)
        G.op("dve", [r_ybank, r_mods[idx]] + [r_x[dc][g] for g in gr], [r_x[dc][g] for g in gr],
             lambda e: e.scalar_tensor_tensor(out=xs[:, dc, t0:t0 + n], in0=ybank[:, 0:n],
                                              scalar=modv(idx, "HG")[:, dc:dc + 1], in1=xs[:, dc, t0:t0 + n],
                                              op0=ALU.mult, op1=ALU.add))
        if extra_bias:
            G.op("dve", [r_mods[idx]] + [r_x[dc][g] for g in gr], [r_x[dc][g] for g in gr],
                 lambda e: e.tensor_scalar(out=xs[:, dc, t0:t0 + n], in0=xs[:, dc, t0:t0 + n],
                                           scalar1=modv(idx, "BG")[:, dc:dc + 1], scalar2=None, op0=ALU.add))

    def loc_barrier():
        evs = [(e, G.count[e]) for e in ("pe", "act", "dve") if G.count[e] > 0]
        for e in ("pe", "act", "dve"):
            G.pending[e] = tuple(G.pending.get(e, ())) + tuple(evs)
        return evs

    def ffn(l, s, tiles, nxt):
        idx = l * 3 + s
        fine = [0]
        adajob["fine"] = True
        f = s // 2
        for (t0, n) in tiles:
            norm_tile(idx, t0, n, hF, t0)
        all_h = lambda t0, n: [r_h[kc][g] for kc in range(KC) for g in gran(t0, n)]

        def in_group(g):
            ab = g % 2
            for j in range(4):
                fc = 4 * g + j
                pc, rp = acquire(("fin", l, f, fc))
                for ti, (t0, n) in enumerate(tiles):
                    za, zu = kz[0] % 2 * 2, kz[0] % 2 * 2 + 1
                    kz[0] += 1
                    for half, zi in ((0, za), (1, zu)):
                        def fm(e, pc=pc, half=half, zi=zi, t0=t0, n=n):
                            ins = None
                            for kc in range(KC):
                                ins = e.matmul(zb[zi][:, 0:n],
                                               lhsT=pc[:, kc * 256 + half * 128:kc * 256 + half * 128 + 128],
                                               rhs=hF[:, kc, t0:t0 + n], start=(kc == 0), stop=(kc == KC - 1))
                            return ins
                        G.op("pe", [rp] + all_h(t0, n), [r_zb[zi]], fm)
                    k = kt[0] % 2
                    kt[0] += 1
                    G.op("act", [r_zb[za]], [r_sgt[k]],
                         lambda e, za=za, k=k, n=n: e.activation(out=sgt[k][:, 0:n], in_=zb[za][:, 0:n], func=AF.Silu))
                    G.op("dve", [r_sgt[k], r_zb[zu]], [r_act[ab][j][ti]],
                         lambda e, zu=zu, k=k, n=n, t0=t0, ab=ab, j=j: e.tensor_tensor(
                             out=actb[ab][:, j, t0:t0 + n], in0=sgt[k][:, 0:n], in1=zb[zu][:, 0:n], op=ALU.mult))
                    for _ in range(adajob.get("njobs", 1)):
                        ada_fine()
                mrelease()

        def out_group(g):
            ab = g % 2
            for hf in range(2):
                pc, rp = acquire(("fout", l, f, g, hf))
                for dcl in range(8):
                    dc = hf * 8 + dcl
                    for ti, (t0, n) in enumerate(tiles):
                        ybk, r_ybk = y3[ky3[0] % 3]
                        ky3[0] += 1

                        def fm(e, pc=pc, dcl=dcl, ybk=ybk, t0=t0, n=n, ab=ab):
                            ins = None
                            for j in range(4):
                                ins = e.matmul(ybk[:, 0:n], lhsT=pc[:, j * 1024 + dcl * 128:j * 1024 + dcl * 128 + 128],
                                               rhs=actb[ab][:, j, t0:t0 + n], start=(j == 0), stop=(j == 3))
                            return ins
                        G.op("pe", [rp] + [r_act[ab][j][ti] for j in range(4)], [r_ybk], fm)
                        x_update(idx, dc, t0, n, ybk, r_ybk)
                mrelease()

        for g in range(11):
            in_group(g)
            if g >= 1:
                while not adajob.get("gate_done", True):
                    ada_fine()
                out_group(g - 1)
        out_group(10)
        adajob["fine"] = False

    def gmlp(l, tiles, nxt, bev):
        idx = l * 3 + 1
        r_g = R("gm_const")
        r_vt = [R("vt") for _ in range(2)]
        r_vn = [R("vn") for _ in range(2)]
        r_junk = R("junk")
        r_G = [[R("G") for _ in range(5)] for _ in range(KC)]
        r_ut = [R("ut") for _ in range(2)]
        r_st = R("gstat")
        G.dma("sp", "ld2", [], [r_g], lambda e: e.dma_start(out=gtmp, in_=gc_d), extra=bev)
        for hd in range(16):
            G.op("dve", [r_g], [r_g],
                 lambda e, hd=hd: e.tensor_tensor(out=wsTm[:, hd, :], in0=gtmp[:, hd * 128:(hd + 1) * 128],
                                                  in1=gtmp[:, 2048:2176], op=ALU.mult))
        for q in range(4):
            def fr(e, q=q):
                return e.matmul(mbk[:, :], lhsT=ones1, rhs=wsTm[:, 4 * q:4 * q + 4, :].rearrange("p h t -> p (h t)"),
                                start=True, stop=True)
            G.op("pe", [r_g, r_const], [r_mb], fr)
            for hh in range(4):
                hd = 4 * q + hh
                G.op("dve", [r_mb, r_g, r_sm], [r_g],
                     lambda e, hd=hd, hh=hh: e.scalar_tensor_tensor(
                         out=Qc[:, hd, :], in0=mbk[:, hh * 128:(hh + 1) * 128], scalar=sm("gmlnb", hd, hd + 1),
                         in1=gtmp[:, 2176 + hd * 128:2176 + (hd + 1) * 128], op0=ALU.mult, op1=ALU.add))
        for r in r_vt + r_vn + [r_junk]:
            r.r.append(("dve", G.count["dve"]))
            r.r.append(("pe", G.count["pe"]))

        def gm_group(grp):
            base = grp[0][0]
            offs = [(t0, n, t0 - base) for (t0, n) in grp]
            ng = sum(n for (_, n) in grp)
            nch = ng // 128
            for (t0, n, off) in offs:
                norm_tile(idx, t0, n, hG, off)
            hres = lambda c: [r_h[kc][c] for kc in range(KC)]
            c = 0
            while c < nch:
                cs = [cc for cc in (c, c + 1) if cc < nch]
                for nt in range(8):
                    pc, rp = acquire(("gv2", nt))
                    for ci, cc in enumerate(cs):
                        zi = kz[0] % 4
                        kz[0] += 1

                        def fm(e, pc=pc, cc=cc, zi=zi):
                            ins = None
                            for kc in range(KC):
                                ins = e.matmul(zb[zi][:, 0:256], lhsT=hG[:, kc, cc * 128:(cc + 1) * 128],
                                               rhs=pc[:, kc * 256:(kc + 1) * 256],
                                               start=(kc == 0), stop=(kc == KC - 1))
                            return ins
                        G.op("pe", [rp] + hres(cc), [r_zb[zi]], fm)
                        G.op("act", [r_zb[zi]], [r_vt[ci]],
                             lambda e, zi=zi, ci=ci, nt=nt: e.activation(out=vt[ci][:, nt * 256:(nt + 1) * 256],
                                                                         in_=zb[zi][:, 0:256], func=AF.Gelu))
                    mrelease()
                for ci, cc in enumerate(cs):
                    st = smstat[:, ci * 8:ci * 8 + 8]
                    G.op("dve", [r_vt[ci]], [r_st],
                         lambda e, ci=ci, st=st: e.tensor_reduce(out=st[:, 0:1], in_=vt[ci], axis=AX.X, op=ALU.add))
                    G.op("dve", [r_st], [r_st],
                         lambda e, st=st: e.tensor_scalar(out=st[:, 1:2], in0=st[:, 0:1], scalar1=-1.0 / D,
                                                          scalar2=None, op0=ALU.mult))
                    G.op("dve", [r_st], [r_st], lambda e, st=st: e.memset(st[:, 2:3], 0.0))
                    G.op("act", [r_vt[ci], r_st], [r_junk, r_st],
                         lambda e, ci=ci, st=st: e.activation(out=junk, in_=vt[ci], func=AF.Square, bias=st[:, 1:2],
                                                              scale=1.0, accum_out=st[:, 2:3]))
                    G.op("act", [r_st], [r_st],
                         lambda e, st=st: e.activation(out=st[:, 3:4], in_=st[:, 2:3], func=AF.Sqrt, bias=epsc[:, 0:1],
                                                       scale=1.0 / D))
                    G.op("dve", [r_st], [r_st],
                         lambda e, st=st: e.reciprocal(out=st[:, 3:4], in_=st[:, 3:4]))
                    G.op("dve", [r_st], [r_st],
                         lambda e, st=st: e.tensor_tensor(out=st[:, 4:5], in0=st[:, 1:2], in1=st[:, 3:4], op=ALU.mult))
                    G.op("act", [r_vt[ci], r_st], [r_vn[ci]],
                         lambda e, ci=ci, st=st: e.activation(out=vn[ci], in_=vt[ci], func=AF.Identity,
                                                              bias=st[:, 4:5], scale=st[:, 3:4]))
                    for q in range(4):
                        bk, rbk = (mbk, r_mb) if q % 2 == 0 else (sbk, r_sb)

                        def fs(e, ci=ci, q=q, bk=bk):
                            ins = None
                            for hh in range(4):
                                hd = 4 * q + hh
                                ins = e.matmul(bk[:, hh * 128:(hh + 1) * 128], lhsT=vn[ci][:, hd * 128:(hd + 1) * 128],
                                               rhs=wsTm[:, hd, :], start=True, stop=True)
                            return ins
                        G.op("pe", [r_vn[ci], r_g], [rbk], fs)
                        for hh in range(4):
                            hd = 4 * q + hh
                            G.op("dve", [rbk, r_g, r_sm], [r_G[hd][cc]],
                                 lambda e, hd=hd, hh=hh, cc=cc, bk=bk: e.scalar_tensor_tensor(
                                     out=Gs[:, hd, cc * 128:(cc + 1) * 128], in0=bk[:, hh * 128:(hh + 1) * 128],
                                     scalar=sm("gmlng", hd, hd + 1), in1=Qc[:, hd, :], op0=ALU.mult, op1=ALU.add))
                c += 2
            for ep in range(8):
                pc, rp = acquire(("gu", ep))
                for e2 in range(2):
                    ec = 2 * ep + e2
                    for (t0, n, off) in offs:
                        zi = kz[0] % 4
                        kz[0] += 1
                        hg = gran(off, n)

                        def fm(e, pc=pc, e2=e2, zi=zi, n=n, off=off):
                            ins = None
                            for kc in range(KC):
                                ins = e.matmul(zb[zi][:, 0:n], lhsT=pc[:, e2 * 2048 + kc * 128:e2 * 2048 + (kc + 1) * 128],
                                               rhs=hG[:, kc, off:off + n], start=(kc == 0), stop=(kc == KC - 1))
                            return ins
                        G.op("pe", [rp] + [r_h[kc][g] for kc in range(KC) for g in hg], [r_zb[zi]], fm)
                        k = kt[0] % 2
                        kt[0] += 1
                        G.op("act", [r_zb[zi]], [r_ut[k]],
                             lambda e, zi=zi, k=k, n=n: e.activation(out=utmp[k][:, 0:n], in_=zb[zi][:, 0:n], func=AF.Gelu))
                        G.op("dve", [r_ut[k]] + [r_G[ec][g] for g in hg], [r_G[ec][g] for g in hg],
                             lambda e, ec=ec, k=k, n=n, off=off: e.tensor_tensor(out=Gs[:, ec, off:off + n], in0=utmp[k][:, 0:n],
                                                                                 in1=Gs[:, ec, off:off + n], op=ALU.mult))
                mrelease()
            for dp in range(8):
                pc, rp = acquire(("go", dp))
                for d2 in range(2):
                    dc = 2 * dp + d2
                    for (t0, n, off) in offs:
                        yi = ky[0] % 2
                        ky[0] += 1
                        hg = gran(off, n)

                        def fm(e, pc=pc, d2=d2, yi=yi, n=n, off=off):
                            ins = None
                            for ec in range(KC):
                                ins = e.matmul(yb[yi][:, 0:n], lhsT=pc[:, ec * 256 + d2 * 128:ec * 256 + d2 * 128 + 128],
                                               rhs=Gs[:, ec, off:off + n], start=(ec == 0), stop=(ec == KC - 1))
                            return ins
                        G.op("pe", [rp] + [r_G[ec][g] for ec in range(KC) for g in hg], [r_yb[yi]], fm)
                        x_update(idx, dc, t0, n, yb[yi], r_yb[yi])
                mrelease()

        gm_group(tiles[0:2])
        gm_group(tiles[2:3])

    def conv(l, tiles, nxt):
        idx = l * 3 + 1
        r_c2 = [R("c2") for _ in range(KC)]
        r_yb2 = [R("ybuf") for _ in range(2)]
        r_yt = [R("ytail") for _ in range(KC)]
        r_stmp = [R("stmp") for _ in range(2)]
        r_dg = [R("dg") for _ in range(2)]
        r_cbq = [R("cbq") for _ in range(2)]
        r_s1s = R("s1s")
        r_rc = R("rstdc")

        def cv_group(grp):
            base = grp[0][0]
            halo = len(grp) == 2
            for (t0_, n_) in grp:
                norm_tile(idx, t0_, n_, hC, t0_ - base)
            t0, n = grp[-1]
            offm = t0 - base
            gr = gran(offm, n)

            def inproj(pc, rp, off_, n_):
                za, zg = kz[0] % 2 * 2, kz[0] % 2 * 2 + 1
                kz[0] += 1
                hg = gran(off_, n_)
                for half, zi in ((0, za), (1, zg)):
                    def fm(e, pc=pc, half=half, zi=zi, off_=off_, n_=n_):
                        ins = None
                        for kc in range(KC):
                            ins = e.matmul(zb[zi][:, 0:n_], lhsT=pc[:, kc * 256 + half * 128:kc * 256 + half * 128 + 128],
                                           rhs=hC[:, kc, off_:off_ + n_], start=(kc == 0), stop=(kc == KC - 1))
                        return ins
                    G.op("pe", [rp] + [r_h[kc][g] for kc in range(KC) for g in hg], [r_zb[zi]], fm)
                return za, zg

            def glu(cc, yk, za, zg, n_):
                k = kt[0] % 2
                kt[0] += 1
                G.op("act", [r_zb[zg], r_sm], [r_stmp[k]],
                     lambda e, zg=zg, k=k, cc=cc, n_=n_: e.activation(out=stmp[k][:, 0:n_], in_=zb[zg][:, 0:n_], func=AF.Sigmoid,
                                                                      bias=sm("cvbin", 16 + cc, 17 + cc), scale=1.0))
                G.op("dve", [r_zb[za], r_stmp[k], r_sm], [r_yb2[yk]],
                     lambda e, za=za, k=k, cc=cc, yk=yk, n_=n_: e.scalar_tensor_tensor(
                         out=ybuf[yk][:, 30:30 + n_], in0=zb[za][:, 0:n_], scalar=sm("cvbin", cc, cc + 1),
                         in1=stmp[k][:, 0:n_], op0=ALU.add, op1=ALU.mult))

            def conv_mm(cc):
                yk = cc % 2
                yi = ky[0] % 2
                ky[0] += 1

                def fc(e, yk=yk, yi=yi):
                    ins = None
                    for j in range(31):
                        ins = e.matmul(yb[yi][:, 0:n], lhsT=dg[yk][:, j, :], rhs=ybuf[yk][:, j:j + n],
                                       start=(j == 0), stop=(j == 30))
                    return ins
                G.op("pe", [r_dg[yk], r_yb2[yk]], [r_yb[yi]], fc)
                G.op("act", [r_yb[yi], r_sm], [r_c2[cc]],
                     lambda e, yi=yi, cc=cc: e.activation(out=c2[:, cc, 0:n], in_=yb[yi][:, 0:n], func=AF.Identity,
                                                          bias=sm("dwb", cc, cc + 1), scale=1.0))
                G.op("act", [r_yb[yi], r_sm], [r_cbq[0]],
                     lambda e, yi=yi, cc=cc: e.activation(out=cbq[0][:, 0:n], in_=yb[yi][:, 0:n], func=AF.Identity,
                                                          bias=sm("dwb", cc, cc + 1), scale=1.0))
                G.op("act", [r_yb[yi], r_sm], [r_cbq[1]],
                     lambda e, yi=yi, cc=cc: e.activation(out=cbq[1][:, 0:n], in_=yb[yi][:, 0:n], func=AF.Square,
                                                          bias=sm("dwb", cc, cc + 1), scale=1.0))
                G.op("pe", [r_cbq[0], r_const], [r_sb],
                     lambda e, cc=cc: e.matmul(sbk[:, 0:n], lhsT=onesD, rhs=cbq[0][:, 0:n], start=(cc == 0), stop=(cc == KC - 1)))
                G.op("pe", [r_cbq[1], r_const], [r_mb],
                     lambda e, cc=cc: e.matmul(mbk[:, 0:n], lhsT=onesD, rhs=cbq[1][:, 0:n], start=(cc == 0), stop=(cc == KC - 1)))

            for cc in range(KC):
                yk = cc % 2
                G.op("dve", [r_sm, r_const], [r_dg[yk]],
                     lambda e, cc=cc, yk=yk: e.tensor_tensor(
                         out=dg[yk], in0=identb.unsqueeze(1).to_broadcast([128, 31, 128]),
                         in1=sm("dw", cc * 31, cc * 31 + 31).unsqueeze(2).to_broadcast([128, 31, 128]), op=ALU.mult))
                pc, rp = acquire(("ci", cc))
                if halo:
                    za, zg = inproj(pc, rp, 0, 128)
                    glu(cc, yk, za, zg, 128)
                    G.op("dve", [r_yb2[yk], r_sm], [r_yt[cc]],
                         lambda e, cc=cc, yk=yk: e.tensor_scalar(out=ytail[:, cc, :], in0=ybuf[yk][:, 128:158],
                                                                 scalar1=sm("hmask", 0, 1), scalar2=None, op0=ALU.mult))
                za, zg = inproj(pc, rp, offm, n)
                mrelease()
                glu(cc, yk, za, zg, n)
                G.op("dve", [r_yt[cc]], [r_yb2[yk]],
                     lambda e, cc=cc, yk=yk: e.tensor_copy(out=ybuf[yk][:, 0:30], in_=ytail[:, cc, :]))
                G.op("dve", [r_yb2[yk]], [r_yt[cc]],
                     lambda e, cc=cc, yk=yk: e.tensor_copy(out=ytail[:, cc, :], in_=ybuf[yk][:, n:n + 30]))
                if cc >= 1:
                    conv_mm(cc - 1)
            conv_mm(KC - 1)
            G.op("dve", [r_sb], [r_s1s], lambda e: e.tensor_copy(out=s1s[:, 0:n], in_=sbk[:, 0:n]))
            G.op("dve", [r_s1s], [r_rc],
                 lambda e: e.tensor_tensor(out=rstdc[:, 0:n], in0=s1s[:, 0:n], in1=s1s[:, 0:n], op=ALU.mult))
            G.op("dve", [r_mb, r_rc], [r_rc],
                 lambda e: e.tensor_tensor(out=rstdc[:, 0:n], in0=mbk[:, 0:n], in1=rstdc[:, 0:n], op=ALU.subtract))
            G.op("act", [r_rc], [r_rc],
                 lambda e: e.activation(out=rstdc[:, 0:n], in_=rstdc[:, 0:n], func=AF.Sqrt, bias=epsc[:, 0:1], scale=1.0))
            G.op("dve", [r_rc], [r_rc],
                 lambda e: e.reciprocal(out=rstdc[:, 0:n], in_=rstdc[:, 0:n]))
            for cc in range(KC):
                G.op("dve", [r_c2[cc], r_s1s], [r_c2[cc]],
                     lambda e, cc=cc: e.tensor_tensor(out=c2[:, cc, 0:n], in0=c2[:, cc, 0:n], in1=s1s[:, 0:n], op=ALU.subtract))
                G.op("dve", [r_c2[cc], r_rc], [r_c2[cc]],
                     lambda e, cc=cc: e.tensor_tensor(out=c2[:, cc, 0:n], in0=c2[:, cc, 0:n], in1=rstdc[:, 0:n], op=ALU.mult))
                G.op("act", [r_c2[cc], r_sm], [r_h[cc][g] for g in gr],
                     lambda e, cc=cc: e.activation(out=cact[:, cc, offm:offm + n], in_=c2[:, cc, 0:n], func=AF.Silu,
                                                   scale=sm("cvlng", cc, cc + 1), bias=sm("cvlnb", cc, cc + 1)))
            allh = [r_h[kc][g] for kc in range(KC) for g in gr]
            for dp in range(8):
                pc, rp = acquire(("co", dp))
                for d2 in range(2):
                    dc = 2 * dp + d2
                    yi = ky[0] % 2
                    ky[0] += 1

                    def fm(e, pc=pc, d2=d2, yi=yi):
                        ins = None
                        for cc in range(KC):
                            ins = e.matmul(yb[yi][:, 0:n], lhsT=pc[:, cc * 256 + d2 * 128:cc * 256 + d2 * 128 + 128],
                                           rhs=cact[:, cc, offm:offm + n], start=(cc == 0), stop=(cc == KC - 1))
                        return ins
                    G.op("pe", [rp] + allh, [r_yb[yi]], fm)
                    x_update(idx, dc, t0, n, yb[yi], r_yb[yi], extra_bias=True)
                mrelease()

        cv_group(tiles[0:2])
        cv_group(tiles[2:3])

    def upcoming(i):
        jobs = []
        j = i + 1
        while j < len(subs):
            jobs.append(subs[j])
            if subs[j][1] != 1:
                break
            j += 1
        return jobs

    defer_gate = subs[0][1] != 1
    pro = [subs[0] + ("ss",)] if defer_gate else [subs[0]] + upcoming(0)
    ada_start(pro)
    ada_drain()
    tiles = list(TILES)
    for i, (l, s) in enumerate(subs):
        nxt = subs[i + 1] if i + 1 < len(subs) else None
        bev = loc_barrier()
        jobs = ([subs[0] + ("gate",)] if (i == 0 and defer_gate) else []) + (upcoming(i) if s != 1 else [])
        if i == 0 and defer_gate:
            adajob["gate_done"] = False
        if jobs:
            ada_start(jobs)
        if s != 1:
            ffn(l, s, tiles, nxt)
        elif l == 0:
            gmlp(l, tiles, nxt, bev)
        else:
            conv(l, tiles, nxt)
            tiles = TILES[1:]
        ada_drain()
    if plan is not None:
        assert ringst["acq"] == NP, (ringst["acq"], NP)
    loc_barrier()
    outv = out_d.rearrange("(c p) t -> p c t", p=128)
    for (t0, n) in TILES[1:]:
        if do_final:
            norm_tile(0, t0, n, hF, t0, final=True)
        for q in range(4):
            rd = [r_x[kc][g] for kc in range(4 * q, 4 * q + 4) for g in gran(t0, n)]
            G.dma("sp", "st", rd, [],
                  lambda e, q=q, t0=t0, n=n: e.dma_start(out=outv[:, 4 * q:4 * q + 4, t0 - 128:t0 - 128 + n],
                                                         in_=xs[:, 4 * q:4 * q + 4, t0:t0 + n]))
    G.final_wait("sp", [("st", G.dma_count["st"])])

    semnames = list(Gen.ENG) + sorted(G.dma_count.keys())
    sems = {}
    from contextlib import ExitStack
    with ExitStack() as es:
        for nme in semnames:
            sems[nme] = es.enter_context(nc.semaphore("s_" + nme))
        block = es.enter_context(nc.Block())

        waited_vals = {e_: set() for e_ in Gen.ENG}
        for e_ in Gen.ENG:
            for waits, fn, inc in G.ops[e_]:
                for (k, v) in waits:
                    if k in waited_vals:
                        waited_vals[k].add(v)
        rank = {e_: {v: i + 1 for i, v in enumerate(sorted(waited_vals[e_]))} for e_ in Gen.ENG}

        def emit(engname):
            def body(e):
                for waits, fn, inc in G.ops[engname]:
                    for (k, v) in waits:
                        e.wait_ge(sems[k], rank[k][v] if k in rank else v)
                    if fn is None:
                        continue
                    ins = fn(e)
                    if inc[0] in rank:
                        if inc[1] in rank[inc[0]]:
                            ins.then_inc(sems[inc[0]], 1)
                    else:
                        ins.then_inc(sems[inc[0]], inc[1])
            return body
        block.tensor(emit("pe"))
        block.scalar(emit("act"))
        block.vector(emit("dve"))
        block.gpsimd(emit("pool"))
        block.sync(emit("sp"))
    return nc, rec


def _pp(v, ncol):
    return np.ascontiguousarray(np.asarray(v, np.float32).reshape(ncol, 128).T)


def make_in_maps(inp, specs):
    ws = np.empty((len(specs), 128, PIECE), np.float32)
    for i, sp_ in enumerate(specs):
        ws[i] = pack_piece(sp_, inp)
    gconst = np.empty((128, NGC), np.float32)
    gconst[:, 0:2048] = np.transpose(inp["gm_ws"][0], (2, 0, 1)).reshape(128, 2048)
    gconst[:, 2048:2176] = np.triu(np.ones((128, 128), np.float32))
    gconst[:, 2176:] = np.broadcast_to(inp["gm_bs"][0].reshape(1, 2048), (128, 2048))
    maps = []
    for k in range(NCORE):
        b, q = divmod(k, 4)
        sm_ = np.zeros((128, NSM), np.float32)

        def put(name, arr):
            sm_[:, SM[name]:SM[name] + arr.shape[1]] = arr
        put("c", _pp(inp["c"][b], 16))
        put("adab", np.concatenate([_pp(inp["ada_b"][l], 144) for l in range(2)], axis=1))
        put("ng", np.concatenate([_pp(inp["norm_g"][l, s], 16) for l in range(2) for s in range(3)], axis=1))
        put("fg", _pp(inp["final_g"], 16))
        put("cvbin", _pp(inp["cv_b_in"][0], 32))
        dw = inp["cv_dw_w"][0]
        put("dw", np.ascontiguousarray(dw.reshape(31, 16, 128).transpose(2, 1, 0)).reshape(128, 496))
        put("dwb", _pp(inp["cv_dw_b"][0], 16))
        put("cvlng", _pp(inp["cv_ln_g"][0], 16))
        put("cvlnb", _pp(inp["cv_ln_b"][0], 16))
        put("cvbout", _pp(inp["cv_b_out"][0], 16))
        put("gmlng", _pp(inp["gm_ln_g"][0], 16))
        put("gmlnb", _pp(inp["gm_ln_b"][0], 16))
        sm_[:, SM["hmask"]] = 0.0 if q == 0 else 1.0
        sm_[:, SM["ident"]:SM["ident"] + 128] = np.eye(128, dtype=np.float32)
        xT = np.zeros((D, T), np.float32)
        s0 = q * TM
        xT[:, 128:] = inp["x"][b, s0:s0 + TM, :].T
        if q > 0:
            xT[:, :128] = inp["x"][b, s0 - 128:s0, :].T
        maps.append({"xT": xT, "smalls": sm_, "gconst": gconst, "wstream": ws})
    return maps


_CACHE = {}


def run(inp, subs=None, do_final=True, cores=NCORE, trace=False):
    subs = ALL_SUBS if subs is None else subs
    key = (tuple(subs), do_final)
    if key not in _CACHE:
        _CACHE[key] = build_program(subs, do_final)
    nc, specs = _CACHE[key]
    maps = make_in_maps(inp, specs)[:cores]
    res = run_bass_kernel_spmd(nc, maps, core_ids=list(range(cores)), trace=trace)
    outs = [np.ascontiguousarray(r["outT"].T) for r in res.results]
    return outs, res


def kernel(**inputs):
    inp = {k: np.asarray(v) for k, v in inputs.items()}
    outs, _ = run(inp)
    out = np.empty((2, 4096, D), np.float32)
    for k in range(NCORE):
        b, q = divmod(k, 4)
        out[b, q * TM:(q + 1) * TM, :] = outs[k]
    return out
```

```python
import os
import numpy as np
import concourse.bass as bass
import concourse.mybir as mybir
from concourse.bass_utils import run_bass_kernel_spmd

F32 = mybir.dt.float32
BF16 = mybir.dt.bfloat16
AF = mybir.ActivationFunctionType
ALU = mybir.AluOpType
AX = mybir.AxisListType

D = 2048
FF = 5632
NFC = 44
T = 1152
TM = 1024
NCORE = 8
EPS = 1e-6
KC = 16
NS = 4
ADA_STEPS = 2
PIECE = 4096
TILES = [(0, 128), (128, 512), (640, 512)]
ALL_SUBS = [(0, 0), (0, 1), (0, 2), (1, 0), (1, 1), (1, 2)]

SM = {}
_o = 0
for _n, _w in [("c", 16), ("adab", 288), ("ng", 96), ("fg", 16), ("cvbin", 32), ("dw", 496), ("dwb", 16),
               ("cvlng", 16), ("cvlnb", 16), ("cvbout", 16), ("gmlng", 16), ("gmlnb", 16), ("hmask", 1), ("ident", 128)]:
    SM[_n] = _o
    _o += _w
NSM = _o
NGC = 2048 + 128 + 2048


def plan_pieces(subs, halo):
    P = []

    def ada(l, s):
        for cb in range(3):
            for kp in range(8):
                P.append(("ada", l, s, cb, kp))

    def ffn(l, s, nxt):
        f = s // 2
        for g in range(11):
            for j in range(4):
                P.append(("fin", l, f, 4 * g + j))
            if g == 0 and nxt is not None:
                ada(*nxt)
            if g >= 1:
                for hf in range(2):
                    P.append(("fout", l, f, g - 1, hf))
        for hf in range(2):
            P.append(("fout", l, f, 10, hf))

    def gm(nxt):
        first = True
        for (t0, n) in TILES:
            nch = n // 128
            c = 0
            while c < nch:
                for nt in range(4):
                    P.append(("gv", nt, 0))
                    P.append(("gv", nt, 1))
                c += 2
            for ep in range(8):
                P.append(("gu", ep))
            if first and nxt is not None:
                ada(*nxt)
            first = False
            for dp in range(8):
                P.append(("go", dp))

    def cv(nxt):
        first = True
        for ti, (t0, n) in enumerate(TILES):
            for cc in range(16):
                P.append(("ci", cc))
            if first and nxt is not None:
                ada(*nxt)
            first = False
            if ti == 0:
                continue
            for dp in range(8):
                P.append(("co", dp))

    ada(*subs[0])
    for i, (l, s) in enumerate(subs):
        nxt = subs[i + 1] if i + 1 < len(subs) else None
        if s != 1:
            ffn(l, s, nxt)
        elif l == 0:
            gm(nxt)
        else:
            cv(nxt)
    return P


def pack_piece(spec, inp):
    k = spec[0]

    def two_block(W, ca, cb_):
        a = W[:, ca:ca + 128].reshape(16, 128, 128)
        b = W[:, cb_:cb_ + 128].reshape(16, 128, 128)
        return np.concatenate([a, b], axis=2).transpose(1, 0, 2).reshape(128, PIECE)

    if k == "ada":
        _, l, s, cbk, kq = spec
        W = inp["ada_w"][l]
        c0 = s * 6144 + cbk * 1024
        blk = W[kq * 512:(kq + 1) * 512, c0:c0 + 1024].reshape(4, 128, 1024)
        return blk.transpose(1, 0, 2).reshape(128, PIECE)
    if k == "fin":
        _, l, f, fc = spec
        return two_block(inp["ffn_w_in"][l, f], fc * 128, FF + fc * 128)
    if k == "fout":
        _, l, f, g, hf = spec
        W = inp["ffn_w_out"][l, f]
        blk = W[g * 512:(g + 1) * 512, hf * 1024:(hf + 1) * 1024].reshape(4, 128, 1024)
        return blk.transpose(1, 0, 2).reshape(128, PIECE)
    if k == "gv":
        _, nt, kh = spec
        W = inp["gm_w_in"][0]
        blk = W[kh * 1024:(kh + 1) * 1024, 2048 + nt * 512:2048 + (nt + 1) * 512].reshape(8, 128, 512)
        return blk.transpose(1, 0, 2).reshape(128, PIECE)
    if k == "gv2":
        _, nt = spec
        W = inp["gm_w_in"][0]
        blk = W[:, 2048 + nt * 256:2048 + (nt + 1) * 256].reshape(16, 128, 256)
        return blk.transpose(1, 0, 2).reshape(128, PIECE)
    if k == "gu":
        _, ep = spec
        W = inp["gm_w_in"][0]
        blk = W[:, ep * 256:(ep + 1) * 256].reshape(16, 128, 2, 128)
        return blk.transpose(1, 2, 0, 3).reshape(128, PIECE)
    if k == "go":
        _, dp = spec
        return two_block(inp["gm_w_out"][0], dp * 256, dp * 256 + 128)
    if k == "ci":
        _, cc = spec
        return two_block(inp["cv_w_in"][0], cc * 128, 2048 + cc * 128)
    if k == "co":
        _, dp = spec
        return two_block(inp["cv_w_out"][0], dp * 256, dp * 256 + 128)
    raise ValueError(spec)


class Res:
    __slots__ = ("name", "w", "r")

    def __init__(self, name):
        self.name = name
        self.w = None
        self.r = []


class Gen:
    ENG = ("pe", "act", "dve", "pool", "sp")

    def __init__(self):
        self.ops = {e: [] for e in self.ENG}
        self.count = {e: 0 for e in self.ENG}
        self.waited = {e: {} for e in self.ENG}
        self.dma_count = {}
        self.pending = {}

    def _deps(self, eng, reads, writes, extra, skip_same=True):
        need = {}

        def add(ev, same_ok):
            if ev is None:
                return
            k, v = ev
            if k == eng and not same_ok and skip_same:
                return
            if need.get(k, 0) < v:
                need[k] = v
        for r in reads:
            add(r.w, True)
        for w in writes:
            add(w.w, True)
            for ev in w.r:
                add(ev, False)
        for ev in extra:
            add(ev, True)
        for ev in self.pending.pop(eng, ()):
            add(ev, True)
        waits = []
        wd = self.waited[eng]
        for k, v in need.items():
            if wd.get(k, 0) >= v:
                continue
            wd[k] = v
            waits.append((k, v))
        return waits

    def op(self, eng, reads, writes, fn, extra=()):
        waits = self._deps(eng, reads, writes, extra, skip_same=False)
        self.count[eng] += 1
        ev = (eng, self.count[eng])
        self.ops[eng].append((waits, fn, (eng, self.count[eng])))
        for r in reads:
            r.r.append(ev)
        for w in writes:
            w.w = ev
            w.r = []
        return ev

    def dma(self, queue, semname, reads, writes, fn, extra=()):
        waits = self._deps(queue, reads, writes, extra, skip_same=False)
        self.dma_count[semname] = self.dma_count.get(semname, 0) + 16
        ev = (semname, self.dma_count[semname])
        self.ops[queue].append((waits, fn, (semname, 16)))
        for r in reads:
            r.r.append(ev)
        for w in writes:
            w.w = ev
            w.r = []
        return ev

    def final_wait(self, eng, evs):
        waits = self._deps(eng, [], [], evs)
        self.ops[eng].append((waits, None, None))


def build_program(subs, do_final=True):
    _, rec = _build(subs, do_final, None)
    nc, _ = _build(subs, do_final, rec)
    uniq = []
    for sp_ in rec:
        if sp_ not in uniq:
            uniq.append(sp_)
    return nc, uniq


def _build(subs, do_final, plan):
    nc = bass.Bass("TRN2", target_bir_lowering=False)
    rec = []
    uniq = {}
    uid = []
    for sp_ in (plan or []):
        if sp_ not in uniq:
            uniq[sp_] = len(uniq)
        uid.append(uniq[sp_])
    NU = max(1, len(uniq))
    NP = len(plan) if plan is not None else 10 ** 9

    xT_d = nc.dram_tensor("xT", [D, T], F32, kind="ExternalInput").ap()
    sm_d = nc.dram_tensor("smalls", [128, NSM], F32, kind="ExternalInput").ap()
    gc_d = nc.dram_tensor("gconst", [128, NGC], F32, kind="ExternalInput").ap()
    ws_d = nc.dram_tensor("wstream", [NU, 128, PIECE], F32, kind="ExternalInput").ap()
    out_d = nc.dram_tensor("outT", [D, TM], F32, kind="ExternalOutput").ap()

    NB = 106400
    SB = nc.alloc_sbuf_tensor("SB", [128, NB], BF16)
    cur = [0]

    def carve(units, dt=BF16, at=None):
        if at is None:
            at = cur[0]
            cur[0] += (units + 15) // 16 * 16
            assert cur[0] <= NB, ("SBUF overflow", cur[0])
        ap = SB[:, at:at + units]
        if dt == F32:
            ap = ap.bitcast(F32)
        return ap

    xs = carve(KC * T * 2, F32).rearrange("p (c t) -> p c t", c=KC)
    ring = [carve(PIECE) for _ in range(NS)]
    smalls = carve(NSM * 2, F32)
    rstd = carve(512 * 2, F32)
    tmpA = [carve(1024, F32) for _ in range(2)]
    accA = carve(2048, F32)
    condf = carve(32, F32)
    identb = carve(128)
    mods = carve(6 * 96 * 2, F32)
    ones1 = carve(128)
    onesD = carve(128)
    onef = carve(2, F32)
    epsc = carve(2, F32)
    condb = carve(16)
    smstat = carve(64 * 2, F32)
    LOC0 = cur[0]
    LOCN = NB - LOC0

    def loc_alloc():
        st = [LOC0]

        def f(units, dt=BF16):
            at = st[0]
            st[0] += (units + 15) // 16 * 16
            assert st[0] <= NB, ("LOC overflow", st[0] - LOC0, LOCN)
            return carve(units, dt, at=at)
        return f

    la = loc_alloc()
    hF = la(KC * T).rearrange("p (c t) -> p c t", c=KC)
    actb = [la(4 * T).rearrange("p (j t) -> p j t", j=4) for _ in range(2)]
    sgt = [la(1024, F32) for _ in range(2)]
    la = loc_alloc()
    hG = la(KC * 640).rearrange("p (c t) -> p c t", c=KC)
    vt_off = LOC0 + KC * 640
    vt = [la(4096, F32) for _ in range(2)]
    vn = [la(2048) for _ in range(2)]
    junk = la(2048)
    Gs = la(KC * 640).rearrange("p (c t) -> p c t", c=KC)
    Qc = la(2048 * 2, F32).rearrange("p (h t) -> p h t", h=16)
    wsTm = la(2048).rearrange("p (h t) -> p h t", h=16)
    utmp = [la(1024, F32) for _ in range(2)]
    gtmp = carve(NGC * 2, F32, at=vt_off)
    assert NGC * 2 <= 2 * 4096 + 2 * 2048 + 2048
    la = loc_alloc()
    hC = la(KC * 640).rearrange("p (c t) -> p c t", c=KC)
    cact = hC
    c2 = la(KC * 512 * 2, F32).rearrange("p (c t) -> p c t", c=KC)
    ybuf = [la(544) for _ in range(2)]
    ytail = la(KC * 30).rearrange("p (c t) -> p c t", c=KC)
    stmp = [la(1024, F32) for _ in range(2)]
    dg = [la(31 * 128).rearrange("p (j c) -> p j c", j=31) for _ in range(2)]
    cbq = [la(512) for _ in range(2)]
    s1s = la(1024, F32)
    rstdc = la(1024, F32)

    banks = [nc.alloc_psum_tensor("bank%d" % i, [128, 512], F32) for i in range(8)]
    zb = banks[0:4]
    yb = banks[4:6]
    sbk = banks[6]
    mbk = banks[7]

    G = Gen()
    R = lambda n: Res(n)
    r_x = [[R("x") for _ in range(9)] for _ in range(KC)]
    r_h = [[R("h") for _ in range(9)] for _ in range(KC)]
    r_ring = [R("ring%d" % i) for i in range(NS)]
    r_zb = [R("zb") for _ in range(4)]
    r_yb = [R("yb") for _ in range(2)]
    r_sb = R("sb")
    r_mb = R("mb")
    r_sm = R("smalls")
    r_rstd = [R("rstd") for _ in range(9)]
    r_tmpA = [R("tmpA") for _ in range(2)]
    r_acc = R("accA")
    r_mods = [R("mods") for _ in range(6)]
    r_const = R("const")
    r_loc = R("loc")
    r_act = [[[R("act") for _ in range(3)] for _ in range(4)] for _ in range(2)]
    r_sgt = [R("sgt") for _ in range(2)]

    def gran(t0, n):
        return list(range(t0 // 128, (t0 + n) // 128))

    ringst = {"next_dma": 0, "released": 0, "acq": 0}

    def pump():
        while ringst["next_dma"] < NP and ringst["next_dma"] - NS < ringst["released"]:
            j = ringst["next_dma"]
            slot = j % NS
            src = ws_d[uid[j] if plan is not None else 0]
            dst = ring[slot]
            G.dma("pool", "ring%d" % slot, [], [r_ring[slot]],
                  lambda e, dst=dst, src=src: e.dma_start(out=dst, in_=src))
            ringst["next_dma"] += 1

    held = []
    relflag = {}

    def acquire(spec, owner="main"):
        i = ringst["acq"]
        rec.append(spec)
        if plan is not None:
            assert plan[i] == spec, (i, plan[i], spec)
        assert i < ringst["next_dma"], "ring deadlock: piece %d not prefetchable" % i
        ringst["acq"] += 1
        held.append((i, owner))
        slot = i % NS
        return ring[slot], r_ring[slot]

    def release(k=1, owner="main"):
        for _ in range(k):
            for hi, (i, ow) in enumerate(held):
                if ow == owner:
                    held.pop(hi)
                    relflag[i] = True
                    break
            else:
                raise AssertionError("release without held piece for " + owner)
        while relflag.get(ringst["released"], False):
            del relflag[ringst["released"]]
            ringst["released"] += 1
        pump()

    G.dma("sp", "lds", [], [r_sm], lambda e: e.dma_start(out=smalls, in_=sm_d))
    xv = xT_d.rearrange("(c p) t -> p c t", p=128)
    pump()
    xdelay = [r_ring[i].w for i in range(NS) if r_ring[i].w is not None]
    for q in range(4):
        wr = [r_x[kc][g] for kc in range(4 * q, 4 * q + 4) for g in range(9)]
        G.dma("sp", "ld", [], wr,
              lambda e, q=q: e.dma_start(out=xs[:, 4 * q:4 * q + 4, :], in_=xv[:, 4 * q:4 * q + 4, :]))
    tot = ("ld", G.dma_count["ld"])
    for kc in range(KC):
        for g in range(9):
            r_x[kc][g].w = tot
    pump()

    def sm(name, a, b):
        o = SM[name]
        return smalls[:, o + a:o + b]

    G.op("dve", [], [r_const], lambda e: e.memset(ones1, 1.0))
    G.op("dve", [], [r_const], lambda e: e.memset(onesD, 1.0 / D))
    G.op("dve", [], [r_const], lambda e: e.memset(onef, 1.0))
    G.op("dve", [], [r_const], lambda e: e.memset(epsc, EPS))
    G.op("dve", [r_sm], [r_const], lambda e: e.tensor_copy(out=identb, in_=sm("ident", 0, 128)))
    G.op("act", [r_sm], [r_const], lambda e: e.activation(out=condf, in_=sm("c", 0, 16), func=AF.Silu))

    kz = [0]
    ky3 = [0]
    y3 = [(yb[0], r_yb[0]), (yb[1], r_yb[1]), (mbk, r_mb)]
    ky = [0]
    kt = [0]

    def modv(idx, what):
        base = idx * 96
        o = {"shift": 0, "scale": 16, "gate": 32, "A": 48, "HG": 64, "BG": 80}[what]
        return mods[:, base + o:base + o + 16]

    def ada_gen(l, s, eng, part="all"):
        idx = l * 3 + s
        base = idx * 96
        cbks = {"all": range(6), "ss": range(4), "gate": range(4, 6)}[part]
        for cbk in cbks:
            for kq in range(4):
                pc, rp = acquire(("ada", l, s, cbk, kq), owner="ada")
                for kk in range(4):
                    kc = 4 * kq + kk
                    if kc == 0:
                        G.op(eng, [rp, r_const], [r_acc],
                             lambda e, pc=pc: e.tensor_scalar(out=accA, in0=pc[:, 0:1024], scalar1=condf[:, 0:1],
                                                              scalar2=None, op0=ALU.mult))
                    else:
                        G.op(eng, [rp, r_const, r_acc], [r_acc],
                             lambda e, pc=pc, kk=kk, kc=kc: e.scalar_tensor_tensor(
                                 out=accA, in0=pc[:, kk * 1024:(kk + 1) * 1024], scalar=condf[:, kc:kc + 1], in1=accA,
                                 op0=ALU.mult, op1=ALU.add))
                    if kk == 3:
                        release(1, owner="ada")
                    yield False
            yi = ky[0] % 2
            ky[0] += 1

            def f2(e, yi=yi):
                ins = None
                for jj in range(8):
                    ins = e.matmul(yb[yi][:, jj:jj + 1], lhsT=accA[:, jj * 128:(jj + 1) * 128], rhs=onef[:, 0:1],
                                   start=True, stop=True)
                return ins
            G.op("pe", [r_acc, r_const], [r_yb[yi]], f2)
            G.op("dve", [r_yb[yi], r_sm], [r_mods[idx]],
                 lambda e, yi=yi, cbk=cbk: e.tensor_tensor(out=mods[:, base + cbk * 8:base + cbk * 8 + 8], in0=yb[yi][:, 0:8],
                                                           in1=sm("adab", idx * 48 + cbk * 8, idx * 48 + cbk * 8 + 8), op=ALU.add))
        if part in ("all", "ss"):
            G.op("dve", [r_mods[idx], r_sm], [r_mods[idx]],
                 lambda e: e.scalar_tensor_tensor(out=modv(idx, "A"), in0=modv(idx, "scale"), scalar=1.0,
                                                  in1=sm("ng", idx * 16, idx * 16 + 16), op0=ALU.add, op1=ALU.mult))
        if part == "ss":
            yield True
            return
        gs = 1.0 if s == 1 else 0.5
        G.op("dve", [r_mods[idx]], [r_mods[idx]],
             lambda e: e.tensor_scalar(out=modv(idx, "HG"), in0=modv(idx, "gate"), scalar1=gs, scalar2=None,
                                       op0=ALU.mult))
        if (l, s) == (1, 1):
            G.op("dve", [r_mods[idx], r_sm], [r_mods[idx]],
                 lambda e: e.tensor_tensor(out=modv(idx, "BG"), in0=modv(idx, "gate"),
                                           in1=sm("cvbout", 0, 16), op=ALU.mult))
        yield True

    adajob = {"gen": None, "tick": 0}

    def ada_start(jobs, eng="dve"):
        def chain():
            for job in jobs:
                l_, s_ = job[0], job[1]
                part = job[2] if len(job) > 2 else "all"
                for _ in ada_gen(l_, s_, eng, part):
                    yield False
                if part == "gate":
                    adajob["gate_done"] = True
            yield True
        adajob["gen"] = chain()
        adajob["tick"] = 0
        adajob["steps"] = ADA_STEPS * max(1, len(jobs))
        adajob["njobs"] = min(2, max(1, len(jobs)))

    def ada_drain():
        g = adajob["gen"]
        if g is not None:
            adajob["gen"] = None
            for _ in g:
                pass

    def ada_fine():
        g = adajob["gen"]
        if g is None:
            return
        adajob["gen"] = None
        done = next(g)
        adajob["gen"] = None if done else g

    def mrelease(k=1):
        release(k)
        g = adajob["gen"]
        if g is None or adajob.get("fine"):
            return
        adajob["tick"] += 1
        adajob["gen"] = None
        done = False
        for _ in range(adajob["steps"]):
            done = next(g)
            if done:
                break
        adajob["gen"] = None if done else g

    def norm_tile(idx, t0, n, hbuf, hoff, final=False):
        gr = gran(t0, n)
        hgr = gran(hoff, n)
        for kc in range(KC):
            G.op("act", [r_x[kc][g] for g in gr], [r_h[kc][g] for g in hgr],
                 lambda e, kc=kc: e.activation(out=hbuf[:, kc, hoff:hoff + n], in_=xs[:, kc, t0:t0 + n],
                                               func=AF.Square))

            def f(e, kc=kc):
                return e.matmul(sbk[:, 0:n], lhsT=onesD, rhs=hbuf[:, kc, hoff:hoff + n],
                                start=(kc == 0), stop=(kc == KC - 1))
            G.op("pe", [r_h[kc][g] for g in hgr] + [r_const], [r_sb], f)
        G.op("act", [r_sb], [r_rstd[0]],
             lambda e: e.activation(out=rstd[:, 0:n], in_=sbk[:, 0:n], func=AF.Sqrt, bias=epsc[:, 0:1], scale=1.0))
        G.op("dve", [r_rstd[0]], [r_rstd[0]],
             lambda e: e.reciprocal(out=rstd[:, 0:n], in_=rstd[:, 0:n]))
        for kc in range(KC):
            if final:
                G.op("dve", [r_x[kc][g] for g in gr] + [r_rstd[0]] + [r_sm],
                     [r_x[kc][g] for g in gr],
                     lambda e, kc=kc: e.scalar_tensor_tensor(out=xs[:, kc, t0:t0 + n], in0=xs[:, kc, t0:t0 + n],
                                                             scalar=sm("fg", kc, kc + 1), in1=rstd[:, 0:n],
                                                             op0=ALU.mult, op1=ALU.mult))
                continue
            k = kt[0] % 2
            kt[0] += 1
            G.op("dve", [r_x[kc][g] for g in gr] + [r_rstd[0]] + [r_mods[idx]], [r_tmpA[k]],
                 lambda e, kc=kc, k=k: e.scalar_tensor_tensor(out=tmpA[k][:, 0:n], in0=xs[:, kc, t0:t0 + n],
                                                              scalar=modv(idx, "A")[:, kc:kc + 1],
                                                              in1=rstd[:, 0:n], op0=ALU.mult, op1=ALU.mult))
            G.op("act", [r_tmpA[k], r_mods[idx]], [r_h[kc][g] for g in hgr],
                 lambda e, kc=kc, k=k: e.activation(out=hbuf[:, kc, hoff:hoff + n], in_=tmpA[k][:, 0:n],
                                                    func=AF.Identity, bias=modv(idx, "shift")[:, kc:kc + 1],
                                                    scale=1.0))

    def x_update(idx, dc, t0, n, ybank, r_ybank, extra_bias=False):
        gr = gran(t0, n)
        G.op("dve", [r_ybank, r_mods[idx]] + [r_x[dc][g] for g in gr], [r_x[dc][g] for g in gr],
             lambda e: e.scalar_tensor_tensor(out=xs[:, dc, t0:t0 + n], in0=ybank[:, 0:n],
                                              scalar=modv(idx, "HG")[:, dc:dc + 1], in1=xs[:, dc, t0:t0 + n],
                                              op0=ALU.mult, op1=ALU.add))
        if extra_bias:
            G.op("dve", [r_mods[idx]] + [r_x[dc][g] for g in gr], [r_x[dc][g] for g in gr],
                 lambda e: e.tensor_scalar(out=xs[:, dc, t0:t0 + n], in0=xs[:, dc, t0:t0 + n],
                                           scalar1=modv(idx, "BG")[:, dc:dc + 1], scalar2=None, op0=ALU.add))

    def loc_barrier():
        evs = [(e, G.count[e]) for e in ("pe", "act", "dve") if G.count[e] > 0]
        for e in ("pe", "act", "dve"):
            G.pending[e] = tuple(G.pending.get(e, ())) + tuple(evs)
        return evs

    def ffn(l, s, tiles, nxt):
        idx = l * 3 + s
        fine = [0]
        adajob["fine"] = True
        f = s // 2
        for (t0, n) in tiles:
            norm_tile(idx, t0, n, hF, t0)
        all_h = lambda t0, n: [r_h[kc][g] for kc in range(KC) for g in gran(t0, n)]

        def in_group(g):
            ab = g % 2
            for j in range(4):
                fc = 4 * g + j
                pc, rp = acquire(("fin", l, f, fc))
                for ti, (t0, n) in enumerate(tiles):
                    za, zu = kz[0] % 2 * 2, kz[0] % 2 * 2 + 1
                    kz[0] += 1
                    for half, zi in ((0, za), (1, zu)):
                        def fm(e, pc=pc, half=half, zi=zi, t0=t0, n=n):
                            ins = None
                            for kc in range(KC):
                                ins = e.matmul(zb[zi][:, 0:n],
                                               lhsT=pc[:, kc * 256 + half * 128:kc * 256 + half * 128 + 128],
                                               rhs=hF[:, kc, t0:t0 + n], start=(kc == 0), stop=(kc == KC - 1))
                            return ins
                        G.op("pe", [rp] + all_h(t0, n), [r_zb[zi]], fm)
                    k = kt[0] % 2
                    kt[0] += 1
                    G.op("act", [r_zb[za]], [r_sgt[k]],
                         lambda e, za=za, k=k, n=n: e.activation(out=sgt[k][:, 0:n], in_=zb[za][:, 0:n], func=AF.Silu))
                    G.op("dve", [r_sgt[k], r_zb[zu]], [r_act[ab][j][ti]],
                         lambda e, zu=zu, k=k, n=n, t0=t0, ab=ab, j=j: e.tensor_tensor(
                             out=actb[ab][:, j, t0:t0 + n], in0=sgt[k][:, 0:n], in1=zb[zu][:, 0:n], op=ALU.mult))
                    for _ in range(adajob.get("njobs", 1)):
                        ada_fine()
                mrelease()

        def out_group(g):
            ab = g % 2
            for hf in range(2):
                pc, rp = acquire(("fout", l, f, g, hf))
                for dcl in range(8):
                    dc = hf * 8 + dcl
                    for ti, (t0, n) in enumerate(tiles):
                        ybk, r_ybk = y3[ky3[0] % 3]
                        ky3[0] += 1

                        def fm(e, pc=pc, dcl=dcl, ybk=ybk, t0=t0, n=n, ab=ab):
                            ins = None
                            for j in range(4):
                                ins = e.matmul(ybk[:, 0:n], lhsT=pc[:, j * 1024 + dcl * 128:j * 1024 + dcl * 128 + 128],
                                               rhs=actb[ab][:, j, t0:t0 + n], start=(j == 0), stop=(j == 3))
                            return ins
                        G.op("pe", [rp] + [r_act[ab][j][ti] for j in range(4)], [r_ybk], fm)
                        x_update(idx, dc, t0, n, ybk, r_ybk)
                mrelease()

        for g in range(11):
            in_group(g)
            if g >= 1:
                while not adajob.get("gate_done", True):
                    ada_fine()
                out_group(g - 1)
        out_group(10)
        adajob["fine"] = False

    def gmlp(l, tiles, nxt, bev):
        idx = l * 3 + 1
        r_g = R("gm_const")
        r_vt = [R("vt") for _ in range(2)]
        r_vn = [R("vn") for _ in range(2)]
        r_junk = R("junk")
        r_G = [[R("G") for _ in range(5)] for _ in range(KC)]
        r_ut = [R("ut") for _ in range(2)]
        r_st = R("gstat")
        G.dma("sp", "ld2", [], [r_g], lambda e: e.dma_start(out=gtmp, in_=gc_d), extra=bev)
        for hd in range(16):
            G.op("dve", [r_g], [r_g],
                 lambda e, hd=hd: e.tensor_tensor(out=wsTm[:, hd, :], in0=gtmp[:, hd * 128:(hd + 1) * 128],
                                                  in1=gtmp[:, 2048:2176], op=ALU.mult))
        for q in range(4):
            def fr(e, q=q):
                return e.matmul(mbk[:, :], lhsT=ones1, rhs=wsTm[:, 4 * q:4 * q + 4, :].rearrange("p h t -> p (h t)"),
                                start=True, stop=True)
            G.op("pe", [r_g, r_const], [r_mb], fr)
            for hh in range(4):
                hd = 4 * q + hh
                G.op("dve", [r_mb, r_g, r_sm], [r_g],
                     lambda e, hd=hd, hh=hh: e.scalar_tensor_tensor(
                         out=Qc[:, hd, :], in0=mbk[:, hh * 128:(hh + 1) * 128], scalar=sm("gmlnb", hd, hd + 1),
                         in1=gtmp[:, 2176 + hd * 128:2176 + (hd + 1) * 128], op0=ALU.mult, op1=ALU.add))
        for r in r_vt + r_vn + [r_junk]:
            r.r.append(("dve", G.count["dve"]))
            r.r.append(("pe", G.count["pe"]))

        def gm_group(grp):
            base = grp[0][0]
            offs = [(t0, n, t0 - base) for (t0, n) in grp]
            ng = sum(n for (_, n) in grp)
            nch = ng // 128
            for (t0, n, off) in offs:
                norm_tile(idx, t0, n, hG, off)
            hres = lambda c: [r_h[kc][c] for kc in range(KC)]
            c = 0
            while c < nch:
                cs = [cc for cc in (c, c + 1) if cc < nch]
                for nt in range(8):
                    pc, rp = acquire(("gv2", nt))
                    for ci, cc in enumerate(cs):
                        zi = kz[0] % 4
                        kz[0] += 1

                        def fm(e, pc=pc, cc=cc, zi=zi):
                            ins = None
                            for kc in range(KC):
                                ins = e.matmul(zb[zi][:, 0:256], lhsT=hG[:, kc, cc * 128:(cc + 1) * 128],
                                               rhs=pc[:, kc * 256:(kc + 1) * 256],
                                               start=(kc == 0), stop=(kc == KC - 1))
                            return ins
                        G.op("pe", [rp] + hres(cc), [r_zb[zi]], fm)
                        G.op("act", [r_zb[zi]], [r_vt[ci]],
                             lambda e, zi=zi, ci=ci, nt=nt: e.activation(out=vt[ci][:, nt * 256:(nt + 1) * 256],
                                                                         in_=zb[zi][:, 0:256], func=AF.Gelu))
                    mrelease()
                for ci, cc in enumerate(cs):
                    st = smstat[:, ci * 8:ci * 8 + 8]
                    G.op("dve", [r_vt[ci]], [r_st],
                         lambda e, ci=ci, st=st: e.tensor_reduce(out=st[:, 0:1], in_=vt[ci], axis=AX.X, op=ALU.add))
                    G.op("dve", [r_st], [r_st],
                         lambda e, st=st: e.tensor_scalar(out=st[:, 1:2], in0=st[:, 0:1], scalar1=-1.0 / D,
                                                          scalar2=None, op0=ALU.mult))
                    G.op("dve", [r_st], [r_st], lambda e, st=st: e.memset(st[:, 2:3], 0.0))
                    G.op("act", [r_vt[ci], r_st], [r_junk, r_st],
                         lambda e, ci=ci, st=st: e.activation(out=junk, in_=vt[ci], func=AF.Square, bias=st[:, 1:2],
                                                              scale=1.0, accum_out=st[:, 2:3]))
                    G.op("act", [r_st], [r_st],
                         lambda e, st=st: e.activation(out=st[:, 3:4], in_=st[:, 2:3], func=AF.Sqrt, bias=epsc[:, 0:1],
                                                       scale=1.0 / D))
                    G.op("dve", [r_st], [r_st],
                         lambda e, st=st: e.reciprocal(out=st[:, 3:4], in_=st[:, 3:4]))
                    G.op("dve", [r_st], [r_st],
                         lambda e, st=st: e.tensor_tensor(out=st[:, 4:5], in0=st[:, 1:2], in1=st[:, 3:4], op=ALU.mult))
                    G.op("act", [r_vt[ci], r_st], [r_vn[ci]],
                         lambda e, ci=ci, st=st: e.activation(out=vn[ci], in_=vt[ci], func=AF.Identity,
                                                              bias=st[:, 4:5], scale=st[:, 3:4]))
                    for q in range(4):
                        bk, rbk = (mbk, r_mb) if q % 2 == 0 else (sbk, r_sb)

                        def fs(e, ci=ci, q=q, bk=bk):
                            ins = None
                            for hh in range(4):
                                hd = 4 * q + hh
                                ins = e.matmul(bk[:, hh * 128:(hh + 1) * 128], lhsT=vn[ci][:, hd * 128:(hd + 1) * 128],
                                               rhs=wsTm[:, hd, :], start=True, stop=True)
                            return ins
                        G.op("pe", [r_vn[ci], r_g], [rbk], fs)
                        for hh in range(4):
                            hd = 4 * q + hh
                            G.op("dve", [rbk, r_g, r_sm], [r_G[hd][cc]],
                                 lambda e, hd=hd, hh=hh, cc=cc, bk=bk: e.scalar_tensor_tensor(
                                     out=Gs[:, hd, cc * 128:(cc + 1) * 128], in0=bk[:, hh * 128:(hh + 1) * 128],
                                     scalar=sm("gmlng", hd, hd + 1), in1=Qc[:, hd, :], op0=ALU.mult, op1=ALU.add))
                c += 2
            for ep in range(8):
                pc, rp = acquire(("gu", ep))
                for e2 in range(2):
                    ec = 2 * ep + e2
                    for (t0, n, off) in offs:
                        zi = kz[0] % 4
                        kz[0] += 1
                        hg = gran(off, n)

                        def fm(e, pc=pc, e2=e2, zi=zi, n=n, off=off):
                            ins = None
                            for kc in range(KC):
                                ins = e.matmul(zb[zi][:, 0:n], lhsT=pc[:, e2 * 2048 + kc * 128:e2 * 2048 + (kc + 1) * 128],
                                               rhs=hG[:, kc, off:off + n], start=(kc == 0), stop=(kc == KC - 1))
                            return ins
                        G.op("pe", [rp] + [r_h[kc][g] for kc in range(KC) for g in hg], [r_zb[zi]], fm)
                        k = kt[0] % 2
                        kt[0] += 1
                        G.op("act", [r_zb[zi]], [r_ut[k]],
                             lambda e, zi=zi, k=k, n=n: e.activation(out=utmp[k][:, 0:n], in_=zb[zi][:, 0:n], func=AF.Gelu))
                        G.op("dve", [r_ut[k]] + [r_G[ec][g] for g in hg], [r_G[ec][g] for g in hg],
                             lambda e, ec=ec, k=k, n=n, off=off: e.tensor_tensor(out=Gs[:, ec, off:off + n], in0=utmp[k][:, 0:n],
                                                                                 in1=Gs[:, ec, off:off + n], op=ALU.mult))
                mrelease()
            for dp in range(8):
                pc, rp = acquire(("go", dp))
                for d2 in range(2):
                    dc = 2 * dp + d2
                    for (t0, n, off) in offs:
                        yi = ky[0] % 2
                        ky[0] += 1
                        hg = gran(off, n)

                        def fm(e, pc=pc, d2=d2, yi=yi, n=n, off=off):
                            ins = None
                            for ec in range(KC):
                                ins = e.matmul(yb[yi][:, 0:n], lhsT=pc[:, ec * 256 + d2 * 128:ec * 256 + d2 * 128 + 128],
                                               rhs=Gs[:, ec, off:off + n], start=(ec == 0), stop=(ec == KC - 1))
                            return ins
                        G.op("pe", [rp] + [r_G[ec][g] for ec in range(KC) for g in hg], [r_yb[yi]], fm)
                        x_update(idx, dc, t0, n, yb[yi], r_yb[yi])
                mrelease()

        gm_group(tiles[0:2])
        gm_group(tiles[2:3])

    def conv(l, tiles, nxt):
        idx = l * 3 + 1
        r_c2 = [R("c2") for _ in range(KC)]
        r_yb2 = [R("ybuf") for _ in range(2)]
        r_yt = [R("ytail") for _ in range(KC)]
        r_stmp = [R("stmp") for _ in range(2)]
        r_dg = [R("dg") for _ in range(2)]
        r_cbq = [R("cbq") for _ in range(2)]
        r_s1s = R("s1s")
        r_rc = R("rstdc")

        def cv_group(grp):
            base = grp[0][0]
            halo = len(grp) == 2
            for (t0_, n_) in grp:
                norm_tile(idx, t0_, n_, hC, t0_ - base)
            t0, n = grp[-1]
            offm = t0 - base
            gr = gran(offm, n)

            def inproj(pc, rp, off_, n_):
                za, zg = kz[0] % 2 * 2, kz[0] % 2 * 2 + 1
                kz[0] += 1
                hg = gran(off_, n_)
                for half, zi in ((0, za), (1, zg)):
                    def fm(e, pc=pc, half=half, zi=zi, off_=off_, n_=n_):
                        ins = None
                        for kc in range(KC):
                            ins = e.matmul(zb[zi][:, 0:n_], lhsT=pc[:, kc * 256 + half * 128:kc * 256 + half * 128 + 128],
                                           rhs=hC[:, kc, off_:off_ + n_], start=(kc == 0), stop=(kc == KC - 1))
                        return ins
                    G.op("pe", [rp] + [r_h[kc][g] for kc in range(KC) for g in hg], [r_zb[zi]], fm)
                return za, zg

            def glu(cc, yk, za, zg, n_):
                k = kt[0] % 2
                kt[0] += 1
                G.op("act", [r_zb[zg], r_sm], [r_stmp[k]],
                     lambda e, zg=zg, k=k, cc=cc, n_=n_: e.activation(out=stmp[k][:, 0:n_], in_=zb[zg][:, 0:n_], func=AF.Sigmoid,
                                                                      bias=sm("cvbin", 16 + cc, 17 + cc), scale=1.0))
                G.op("dve", [r_zb[za], r_stmp[k], r_sm], [r_yb2[yk]],
                     lambda e, za=za, k=k, cc=cc, yk=yk, n_=n_: e.scalar_tensor_tensor(
                         out=ybuf[yk][:, 30:30 + n_], in0=zb[za][:, 0:n_], scalar=sm("cvbin", cc, cc + 1),
                         in1=stmp[k][:, 0:n_], op0=ALU.add, op1=ALU.mult))

            def conv_mm(cc):
                yk = cc % 2
                yi = ky[0] % 2
                ky[0] += 1

                def fc(e, yk=yk, yi=yi):
                    ins = None
                    for j in range(31):
                        ins = e.matmul(yb[yi][:, 0:n], lhsT=dg[yk][:, j, :], rhs=ybuf[yk][:, j:j + n],
                                       start=(j == 0), stop=(j == 30))
                    return ins
                G.op("pe", [r_dg[yk], r_yb2[yk]], [r_yb[yi]], fc)
                G.op("act", [r_yb[yi], r_sm], [r_c2[cc]],
                     lambda e, yi=yi, cc=cc: e.activation(out=c2[:, cc, 0:n], in_=yb[yi][:, 0:n], func=AF.Identity,
                                                          bias=sm("dwb", cc, cc + 1), scale=1.0))
                G.op("act", [r_yb[yi], r_sm], [r_cbq[0]],
                     lambda e, yi=yi, cc=cc: e.activation(out=cbq[0][:, 0:n], in_=yb[yi][:, 0:n], func=AF.Identity,
                                                          bias=sm("dwb", cc, cc + 1), scale=1.0))
                G.op("act", [r_yb[yi], r_sm], [r_cbq[1]],
                     lambda e, yi=yi, cc=cc: e.activation(out=cbq[1][:, 0:n], in_=yb[yi][:, 0:n], func=AF.Square,
                                                          bias=sm("dwb", cc, cc + 1), scale=1.0))
                G.op("pe", [r_cbq[0], r_const], [r_sb],
                     lambda e, cc=cc: e.matmul(sbk[:, 0:n], lhsT=onesD, rhs=cbq[0][:, 0:n], start=(cc == 0), stop=(cc == KC - 1)))
                G.op("pe", [r_cbq[1], r_const], [r_mb],
                     lambda e, cc=cc: e.matmul(mbk[:, 0:n], lhsT=onesD, rhs=cbq[1][:, 0:n], start=(cc == 0), stop=(cc == KC - 1)))

            for cc in range(KC):
                yk = cc % 2
                G.op("dve", [r_sm, r_const], [r_dg[yk]],
                     lambda e, cc=cc, yk=yk: e.tensor_tensor(
                         out=dg[yk], in0=identb.unsqueeze(1).to_broadcast([128, 31, 128]),
                         in1=sm("dw", cc * 31, cc * 31 + 31).unsqueeze(2).to_broadcast([128, 31, 128]), op=ALU.mult))
                pc, rp = acquire(("ci", cc))
                if halo:
                    za, zg = inproj(pc, rp, 0, 128)
                    glu(cc, yk, za, zg, 128)
                    G.op("dve", [r_yb2[yk], r_sm], [r_yt[cc]],
                         lambda e, cc=cc, yk=yk: e.tensor_scalar(out=ytail[:, cc, :], in0=ybuf[yk][:, 128:158],
                                                                 scalar1=sm("hmask", 0, 1), scalar2=None, op0=ALU.mult))
                za, zg = inproj(pc, rp, offm, n)
                mrelease()
                glu(cc, yk, za, zg, n)
                G.op("dve", [r_yt[cc]], [r_yb2[yk]],
                     lambda e, cc=cc, yk=yk: e.tensor_copy(out=ybuf[yk][:, 0:30], in_=ytail[:, cc, :]))
                G.op("dve", [r_yb2[yk]], [r_yt[cc]],
                     lambda e, cc=cc, yk=yk: e.tensor_copy(out=ytail[:, cc, :], in_=ybuf[yk][:, n:n + 30]))
                if cc >= 1:
                    conv_mm(cc - 1)
            conv_mm(KC - 1)
            G.op("dve", [r_sb], [r_s1s], lambda e: e.tensor_copy(out=s1s[:, 0:n], in_=sbk[:, 0:n]))
            G.op("dve", [r_s1s], [r_rc],
                 lambda e: e.tensor_tensor(out=rstdc[:, 0:n], in0=s1s[:, 0:n], in1=s1s[:, 0:n], op=ALU.mult))
            G.op("dve", [r_mb, r_rc], [r_rc],
                 lambda e: e.tensor_tensor(out=rstdc[:, 0:n], in0=mbk[:, 0:n], in1=rstdc[:, 0:n], op=ALU.subtract))
            G.op("act", [r_rc], [r_rc],
                 lambda e: e.activation(out=rstdc[:, 0:n], in_=rstdc[:, 0:n], func=AF.Sqrt, bias=epsc[:, 0:1], scale=1.0))
            G.op("dve", [r_rc], [r_rc],
                 lambda e: e.reciprocal(out=rstdc[:, 0:n], in_=rstdc[:, 0:n]))
            for cc in range(KC):
                G.op("dve", [r_c2[cc], r_s1s], [r_c2[cc]],
                     lambda e, cc=cc: e.tensor_tensor(out=c2[:, cc, 0:n], in0=c2[:, cc, 0:n], in1=s1s[:, 0:n], op=ALU.subtract))
                G.op("dve", [r_c2[cc], r_rc], [r_c2[cc]],
                     lambda e, cc=cc: e.tensor_tensor(out=c2[:, cc, 0:n], in0=c2[:, cc, 0:n], in1=rstdc[:, 0:n], op=ALU.mult))
                G.op("act", [r_c2[cc], r_sm], [r_h[cc][g] for g in gr],
                     lambda e, cc=cc: e.activation(out=cact[:, cc, offm:offm + n], in_=c2[:, cc, 0:n], func=AF.Silu,
                                                   scale=sm("cvlng", cc, cc + 1), bias=sm("cvlnb", cc, cc + 1)))
            allh = [r_h[kc][g] for kc in range(KC) for g in gr]
            for dp in range(8):
                pc, rp = acquire(("co", dp))
                for d2 in range(2):
                    dc = 2 * dp + d2
                    yi = ky[0] % 2
                    ky[0] += 1

                    def fm(e, pc=pc, d2=d2, yi=yi):
                        ins = None
                        for cc in range(KC):
                            ins = e.matmul(yb[yi][:, 0:n], lhsT=pc[:, cc * 256 + d2 * 128:cc * 256 + d2 * 128 + 128],
                                           rhs=cact[:, cc, offm:offm + n], start=(cc == 0), stop=(cc == KC - 1))
                        return ins
                    G.op("pe", [rp] + allh, [r_yb[yi]], fm)
                    x_update(idx, dc, t0, n, yb[yi], r_yb[yi], extra_bias=True)
                mrelease()

        cv_group(tiles[0:2])
        cv_group(tiles[2:3])

    def upcoming(i):
        jobs = []
        j = i + 1
        while j < len(subs):
            jobs.append(subs[j])
            if subs[j][1] != 1:
                break
            j += 1
        return jobs

    defer_gate = subs[0][1] != 1
    pro = [subs[0] + ("ss",)] if defer_gate else [subs[0]] + upcoming(0)
    ada_start(pro)
    ada_drain()
    tiles = list(TILES)
    for i, (l, s) in enumerate(subs):
        nxt = subs[i + 1] if i + 1 < len(subs) else None
        bev = loc_barrier()
        jobs = ([subs[0] + ("gate",)] if (i == 0 and defer_gate) else []) + (upcoming(i) if s != 1 else [])
        if i == 0 and defer_gate:
            adajob["gate_done"] = False
        if jobs:
            ada_start(jobs)
        if s != 1:
            ffn(l, s, tiles, nxt)
        elif l == 0:
            gmlp(l, tiles, nxt, bev)
        else:
            conv(l, tiles, nxt)
            tiles = TILES[1:]
        ada_drain()
    if plan is not None:
        assert ringst["acq"] == NP, (ringst["acq"], NP)
    loc_barrier()
    outv = out_d.rearrange("(c p) t -> p c t", p=128)
    for (t0, n) in TILES[1:]:
        if do_final:
            norm_tile(0, t0, n, hF, t0, final=True)
        for q in range(4):
            rd = [r_x[kc][g] for kc in range(4 * q, 4 * q + 4) for g in gran(t0, n)]
            G.dma("sp", "st", rd, [],
                  lambda e, q=q, t0=t0, n=n: e.dma_start(out=outv[:, 4 * q:4 * q + 4, t0 - 128:t0 - 128 + n],
                                                         in_=xs[:, 4 * q:4 * q + 4, t0:t0 + n]))
    G.final_wait("sp", [("st", G.dma_count["st"])])

    semnames = list(Gen.ENG) + sorted(G.dma_count.keys())
    sems = {}
    from contextlib import ExitStack
    with ExitStack() as es:
        for nme in semnames:
            sems[nme] = es.enter_context(nc.semaphore("s_" + nme))
        block = es.enter_context(nc.Block())

        waited_vals = {e_: set() for e_ in Gen.ENG}
        for e_ in Gen.ENG:
            for waits, fn, inc in G.ops[e_]:
                for (k, v) in waits:
                    if k in waited_vals:
                        waited_vals[k].add(v)
        rank = {e_: {v: i + 1 for i, v in enumerate(sorted(waited_vals[e_]))} for e_ in Gen.ENG}

        def emit(engname):
            def body(e):
                for waits, fn, inc in G.ops[engname]:
                    for (k, v) in waits:
                        e.wait_ge(sems[k], rank[k][v] if k in rank else v)
                    if fn is None:
                        continue
                    ins = fn(e)
                    if inc[0] in rank:
                        if inc[1] in rank[inc[0]]:
                            ins.then_inc(sems[inc[0]], 1)
                    else:
                        ins.then_inc(sems[inc[0]], inc[1])
            return body
        block.tensor(emit("pe"))
        block.scalar(emit("act"))
        block.vector(emit("dve"))
        block.gpsimd(emit("pool"))
        block.sync(emit("sp"))
    return nc, rec


def _pp(v, ncol):
    return np.ascontiguousarray(np.asarray(v, np.float32).reshape(ncol, 128).T)


def make_in_maps(inp, specs):
    ws = np.empty((len(specs), 128, PIECE), np.float32)
    for i, sp_ in enumerate(specs):
        ws[i] = pack_piece(sp_, inp)
    gconst = np.empty((128, NGC), np.float32)
    gconst[:, 0:2048] = np.transpose(inp["gm_ws"][0], (2, 0, 1)).reshape(128, 2048)
    gconst[:, 2048:2176] = np.triu(np.ones((128, 128), np.float32))
    gconst[:, 2176:] = np.broadcast_to(inp["gm_bs"][0].reshape(1, 2048), (128, 2048))
    maps = []
    for k in range(NCORE):
        b, q = divmod(k, 4)
        sm_ = np.zeros((128, NSM), np.float32)

        def put(name, arr):
            sm_[:, SM[name]:SM[name] + arr.shape[1]] = arr
        put("c", _pp(inp["c"][b], 16))
        put("adab", np.concatenate([_pp(inp["ada_b"][l], 144) for l in range(2)], axis=1))
        put("ng", np.concatenate([_pp(inp["norm_g"][l, s], 16) for l in range(2) for s in range(3)], axis=1))
        put("fg", _pp(inp["final_g"], 16))
        put("cvbin", _pp(inp["cv_b_in"][0], 32))
        dw = inp["cv_dw_w"][0]
        put("dw", np.ascontiguousarray(dw.reshape(31, 16, 128).transpose(2, 1, 0)).reshape(128, 496))
        put("dwb", _pp(inp["cv_dw_b"][0], 16))
        put("cvlng", _pp(inp["cv_ln_g"][0], 16))
        put("cvlnb", _pp(inp["cv_ln_b"][0], 16))
        put("cvbout", _pp(inp["cv_b_out"][0], 16))
        put("gmlng", _pp(inp["gm_ln_g"][0], 16))
        put("gmlnb", _pp(inp["gm_ln_b"][0], 16))
        sm_[:, SM["hmask"]] = 0.0 if q == 0 else 1.0
        sm_[:, SM["ident"]:SM["ident"] + 128] = np.eye(128, dtype=np.float32)
        xT = np.zeros((D, T), np.float32)
        s0 = q * TM
        xT[:, 128:] = inp["x"][b, s0:s0 + TM, :].T
        if q > 0:
            xT[:, :128] = inp["x"][b, s0 - 128:s0, :].T
        maps.append({"xT": xT, "smalls": sm_, "gconst": gconst, "wstream": ws})
    return maps


_CACHE = {}


def run(inp, subs=None, do_final=True, cores=NCORE, trace=False):
    subs = ALL_SUBS if subs is None else subs
    key = (tuple(subs), do_final)
    if key not in _CACHE:
        _CACHE[key] = build_program(subs, do_final)
    nc, specs = _CACHE[key]
    maps = make_in_maps(inp, specs)[:cores]
    res = run_bass_kernel_spmd(nc, maps, core_ids=list(range(cores)), trace=trace)
    outs = [np.ascontiguousarray(r["outT"].T) for r in res.results]
    return outs, res


def kernel(**inputs):
    inp = {k: np.asarray(v) for k, v in inputs.items()}
    outs, _ = run(inp)
    out = np.empty((2, 4096, D), np.float32)
    for k in range(NCORE):
        b, q = divmod(k, 4)
        out[b, q * TM:(q + 1) * TM, :] = outs[k]
    return out
```

```python
import os
import numpy as np
import concourse.bass as bass
import concourse.mybir as mybir
from concourse.bass_utils import run_bass_kernel_spmd

F32 = mybir.dt.float32
BF16 = mybir.dt.bfloat16
AF = mybir.ActivationFunctionType
ALU = mybir.AluOpType
AX = mybir.AxisListType

D = 2048
FF = 5632
NFC = 44
T = 1152
TM = 1024
NCORE = 8
EPS = 1e-6
KC = 16
NS = 4
ADA_STEPS = 2
PIECE = 4096
TILES = [(0, 128), (128, 512), (640, 512)]
ALL_SUBS = [(0, 0), (0, 1), (0, 2), (1, 0), (1, 1), (1, 2)]

SM = {}
_o = 0
for _n, _w in [("c", 16), ("adab", 288), ("ng", 96), ("fg", 16), ("cvbin", 32), ("dw", 496), ("dwb", 16),
               ("cvlng", 16), ("cvlnb", 16), ("cvbout", 16), ("gmlng", 16), ("gmlnb", 16), ("hmask", 1), ("ident", 128)]:
    SM[_n] = _o
    _o += _w
NSM = _o
NGC = 2048 + 128 + 2048


def plan_pieces(subs, halo):
    P = []

    def ada(l, s):
        for cb in range(3):
            for kp in range(8):
                P.append(("ada", l, s, cb, kp))

    def ffn(l, s, nxt):
        f = s // 2
        for g in range(11):
            for j in range(4):
                P.append(("fin", l, f, 4 * g + j))
            if g == 0 and nxt is not None:
                ada(*nxt)
            if g >= 1:
                for hf in range(2):
                    P.append(("fout", l, f, g - 1, hf))
        for hf in range(2):
            P.append(("fout", l, f, 10, hf))

    def gm(nxt):
        first = True
        for (t0, n) in TILES:
            nch = n // 128
            c = 0
            while c < nch:
                for nt in range(4):
                    P.append(("gv", nt, 0))
                    P.append(("gv", nt, 1))
                c += 2
            for ep in range(8):
                P.append(("gu", ep))
            if first and nxt is not None:
                ada(*nxt)
            first = False
            for dp in range(8):
                P.append(("go", dp))

    def cv(nxt):
        first = True
        for ti, (t0, n) in enumerate(TILES):
            for cc in range(16):
                P.append(("ci", cc))
            if first and nxt is not None:
                ada(*nxt)
            first = False
            if ti == 0:
                continue
            for dp in range(8):
                P.append(("co", dp))

    ada(*subs[0])
    for i, (l, s) in enumerate(subs):
        nxt = subs[i + 1] if i + 1 < len(subs) else None
        if s != 1:
            ffn(l, s, nxt)
        elif l == 0:
            gm(nxt)
        else:
            cv(nxt)
    return P


def pack_piece(spec, inp):
    k = spec[0]

    def two_block(W, ca, cb_):
        a = W[:, ca:ca + 128].reshape(16, 128, 128)
        b = W[:, cb_:cb_ + 128].reshape(16, 128, 128)
        return np.concatenate([a, b], axis=2).transpose(1, 0, 2).reshape(128, PIECE)

    if k == "ada":
        _, l, s, cbk, kq = spec
        W = inp["ada_w"][l]
        c0 = s * 6144 + cbk * 1024
        blk = W[kq * 512:(kq + 1) * 512, c0:c0 + 1024].reshape(4, 128, 1024)
        return blk.transpose(1, 0, 2).reshape(128, PIECE)
    if k == "fin":
        _, l, f, fc = spec
        return two_block(inp["ffn_w_in"][l, f], fc * 128, FF + fc * 128)
    if k == "fout":
        _, l, f, g, hf = spec
        W = inp["ffn_w_out"][l, f]
        blk = W[g * 512:(g + 1) * 512, hf * 1024:(hf + 1) * 1024].reshape(4, 128, 1024)
        return blk.transpose(1, 0, 2).reshape(128, PIECE)
    if k == "gv":
        _, nt, kh = spec
        W = inp["gm_w_in"][0]
        blk = W[kh * 1024:(kh + 1) * 1024, 2048 + nt * 512:2048 + (nt + 1) * 512].reshape(8, 128, 512)
        return blk.transpose(1, 0, 2).reshape(128, PIECE)
    if k == "gv2":
        _, nt = spec
        W = inp["gm_w_in"][0]
        blk = W[:, 2048 + nt * 256:2048 + (nt + 1) * 256].reshape(16, 128, 256)
        return blk.transpose(1, 0, 2).reshape(128, PIECE)
    if k == "gu":
        _, ep = spec
        W = inp["gm_w_in"][0]
        blk = W[:, ep * 256:(ep + 1) * 256].reshape(16, 128, 2, 128)
        return blk.transpose(1, 2, 0, 3).reshape(128, PIECE)
    if k == "go":
        _, dp = spec
        return two_block(inp["gm_w_out"][0], dp * 256, dp * 256 + 128)
    if k == "ci":
        _, cc = spec
        return two_block(inp["cv_w_in"][0], cc * 128, 2048 + cc * 128)
    if k == "co":
        _, dp = spec
        return two_block(inp["cv_w_out"][0], dp * 256, dp * 256 + 128)
    raise ValueError(spec)


class Res:
    __slots__ = ("name", "w", "r")

    def __init__(self, name):
        self.name = name
        self.w = None
        self.r = []


class Gen:
    ENG = ("pe", "act", "dve", "pool", "sp")

    def __init__(self):
        self.ops = {e: [] for e in self.ENG}
        self.count = {e: 0 for e in self.ENG}
        self.waited = {e: {} for e in self.ENG}
        self.dma_count = {}
        self.pending = {}

    def _deps(self, eng, reads, writes, extra, skip_same=True):
        need = {}

        def add(ev, same_ok):
            if ev is None:
                return
            k, v = ev
            if k == eng and not same_ok and skip_same:
                return
            if need.get(k, 0) < v:
                need[k] = v
        for r in reads:
            add(r.w, True)
        for w in writes:
            add(w.w, True)
            for ev in w.r:
                add(ev, False)
        for ev in extra:
            add(ev, True)
        for ev in self.pending.pop(eng, ()):
            add(ev, True)
        waits = []
        wd = self.waited[eng]
        for k, v in need.items():
            if wd.get(k, 0) >= v:
                continue
            wd[k] = v
            waits.append((k, v))
        return waits

    def op(self, eng, reads, writes, fn, extra=()):
        waits = self._deps(eng, reads, writes, extra, skip_same=False)
        self.count[eng] += 1
        ev = (eng, self.count[eng])
        self.ops[eng].append((waits, fn, (eng, self.count[eng])))
        for r in reads:
            r.r.append(ev)
        for w in writes:
            w.w = ev
            w.r = []
        return ev

    def dma(self, queue, semname, reads, writes, fn, extra=()):
        waits = self._deps(queue, reads, writes, extra, skip_same=False)
        self.dma_count[semname] = self.dma_count.get(semname, 0) + 16
        ev = (semname, self.dma_count[semname])
        self.ops[queue].append((waits, fn, (semname, 16)))
        for r in reads:
            r.r.append(ev)
        for w in writes:
            w.w = ev
            w.r = []
        return ev

    def final_wait(self, eng, evs):
        waits = self._deps(eng, [], [], evs)
        self.ops[eng].append((waits, None, None))


def build_program(subs, do_final=True):
    _, rec = _build(subs, do_final, None)
    nc, _ = _build(subs, do_final, rec)
    uniq = []
    for sp_ in rec:
        if sp_ not in uniq:
            uniq.append(sp_)
    return nc, uniq


def _build(subs, do_final, plan):
    nc = bass.Bass("TRN2", target_bir_lowering=False)
    rec = []
    uniq = {}
    uid = []
    for sp_ in (plan or []):
        if sp_ not in uniq:
            uniq[sp_] = len(uniq)
        uid.append(uniq[sp_])
    NU = max(1, len(uniq))
    NP = len(plan) if plan is not None else 10 ** 9

    xT_d = nc.dram_tensor("xT", [D, T], F32, kind="ExternalInput").ap()
    sm_d = nc.dram_tensor("smalls", [128, NSM], F32, kind="ExternalInput").ap()
    gc_d = nc.dram_tensor("gconst", [128, NGC], F32, kind="ExternalInput").ap()
    ws_d = nc.dram_tensor("wstream", [NU, 128, PIECE], F32, kind="ExternalInput").ap()
    out_d = nc.dram_tensor("outT", [D, TM], F32, kind="ExternalOutput").ap()

    NB = 106400
    SB = nc.alloc_sbuf_tensor("SB", [128, NB], BF16)
    cur = [0]

    def carve(units, dt=BF16, at=None):
        if at is None:
            at = cur[0]
            cur[0] += (units + 15) // 16 * 16
            assert cur[0] <= NB, ("SBUF overflow", cur[0])
        ap = SB[:, at:at + units]
        if dt == F32:
            ap = ap.bitcast(F32)
        return ap

    xs = carve(KC * T * 2, F32).rearrange("p (c t) -> p c t", c=KC)
    ring = [carve(PIECE) for _ in range(NS)]
    smalls = carve(NSM * 2, F32)
    rstd = carve(512 * 2, F32)
    tmpA = [carve(1024, F32) for _ in range(2)]
    accA = carve(2048, F32)
    condf = carve(32, F32)
    identb = carve(128)
    mods = carve(6 * 96 * 2, F32)
    ones1 = carve(128)
    onesD = carve(128)
    onef = carve(2, F32)
    epsc = carve(2, F32)
    condb = carve(16)
    smstat = carve(64 * 2, F32)
    LOC0 = cur[0]
    LOCN = NB - LOC0

    def loc_alloc():
        st = [LOC0]

        def f(units, dt=BF16):
            at = st[0]
            st[0] += (units + 15) // 16 * 16
            assert st[0] <= NB, ("LOC overflow", st[0] - LOC0, LOCN)
            return carve(units, dt, at=at)
        return f

    la = loc_alloc()
    hF = la(KC * T).rearrange("p (c t) -> p c t", c=KC)
    actb = [la(4 * T).rearrange("p (j t) -> p j t", j=4) for _ in range(2)]
    sgt = [la(1024, F32) for _ in range(2)]
    la = loc_alloc()
    hG = la(KC * 640).rearrange("p (c t) -> p c t", c=KC)
    vt_off = LOC0 + KC * 640
    vt = [la(4096, F32) for _ in range(2)]
    vn = [la(2048) for _ in range(2)]
    junk = la(2048)
    Gs = la(KC * 640).rearrange("p (c t) -> p c t", c=KC)
    Qc = la(2048 * 2, F32).rearrange("p (h t) -> p h t", h=16)
    wsTm = la(2048).rearrange("p (h t) -> p h t", h=16)
    utmp = [la(1024, F32) for _ in range(2)]
    gtmp = carve(NGC * 2, F32, at=vt_off)
    assert NGC * 2 <= 2 * 4096 + 2 * 2048 + 2048
    la = loc_alloc()
    hC = la(KC * 640).rearrange("p (c t) -> p c t", c=KC)
    cact = hC
    c2 = la(KC * 512 * 2, F32).rearrange("p (c t) -> p c t", c=KC)
    ybuf = [la(544) for _ in range(2)]
    ytail = la(KC * 30).rearrange("p (c t) -> p c t", c=KC)
    stmp = [la(1024, F32) for _ in range(2)]
    dg = [la(31 * 128).rearrange("p (j c) -> p j c", j=31) for _ in range(2)]
    cbq = [la(512) for _ in range(2)]
    s1s = la(1024, F32)
    rstdc = la(1024, F32)

    banks = [nc.alloc_psum_tensor("bank%d" % i, [128, 512], F32) for i in range(8)]
    zb = banks[0:4]
    yb = banks[4:6]
    sbk = banks[6]
    mbk = banks[7]

    G = Gen()
    R = lambda n: Res(n)
    r_x = [[R("x") for _ in range(9)] for _ in range(KC)]
    r_h = [[R("h") for _ in range(9)] for _ in range(KC)]
    r_ring = [R("ring%d" % i) for i in range(NS)]
    r_zb = [R("zb") for _ in range(4)]
    r_yb = [R("yb") for _ in range(2)]
    r_sb = R("sb")
    r_mb = R("mb")
    r_sm = R("smalls")
    r_rstd = [R("rstd") for _ in range(9)]
    r_tmpA = [R("tmpA") for _ in range(2)]
    r_acc = R("accA")
    r_mods = [R("mods") for _ in range(6)]
    r_const = R("const")
    r_loc = R("loc")
    r_act = [[[R("act") for _ in range(3)] for _ in range(4)] for _ in range(2)]
    r_sgt = [R("sgt") for _ in range(2)]

    def gran(t0, n):
        return list(range(t0 // 128, (t0 + n) // 128))

    ringst = {"next_dma": 0, "released": 0, "acq": 0}

    def pump():
        while ringst["next_dma"] < NP and ringst["next_dma"] - NS < ringst["released"]:
            j = ringst["next_dma"]
            slot = j % NS
            src = ws_d[uid[j] if plan is not None else 0]
            dst = ring[slot]
            G.dma("pool", "ring%d" % slot, [], [r_ring[slot]],
                  lambda e, dst=dst, src=src: e.dma_start(out=dst, in_=src))
            ringst["next_dma"] += 1

    held = []
    relflag = {}

    def acquire(spec, owner="main"):
        i = ringst["acq"]
        rec.append(spec)
        if plan is not None:
            assert plan[i] == spec, (i, plan[i], spec)
        assert i < ringst["next_dma"], "ring deadlock: piece %d not prefetchable" % i
        ringst["acq"] += 1
        held.append((i, owner))
        slot = i % NS
        return ring[slot], r_ring[slot]

    def release(k=1, owner="main"):
        for _ in range(k):
            for hi, (i, ow) in enumerate(held):
                if ow == owner:
                    held.pop(hi)
                    relflag[i] = True
                    break
            else:
                raise AssertionError("release without held piece for " + owner)
        while relflag.get(ringst["released"], False):
            del relflag[ringst["released"]]
            ringst["released"] += 1
        pump()

    G.dma("sp", "lds", [], [r_sm], lambda e: e.dma_start(out=smalls, in_=sm_d))
    xv = xT_d.rearrange("(c p) t -> p c t", p=128)
    pump()
    xdelay = [r_ring[i].w for i in range(NS) if r_ring[i].w is not None]
    for q in range(4):
        wr = [r_x[kc][g] for kc in range(4 * q, 4 * q + 4) for g in range(9)]
        G.dma("sp", "ld", [], wr,
              lambda e, q=q: e.dma_start(out=xs[:, 4 * q:4 * q + 4, :], in_=xv[:, 4 * q:4 * q + 4, :]))
    tot = ("ld", G.dma_count["ld"])
    for kc in range(KC):
        for g in range(9):
            r_x[kc][g].w = tot
    pump()

    def sm(name, a, b):
        o = SM[name]
        return smalls[:, o + a:o + b]

    G.op("dve", [], [r_const], lambda e: e.memset(ones1, 1.0))
    G.op("dve", [], [r_const], lambda e: e.memset(onesD, 1.0 / D))
    G.op("dve", [], [r_const], lambda e: e.memset(onef, 1.0))
    G.op("dve", [], [r_const], lambda e: e.memset(epsc, EPS))
    G.op("dve", [r_sm], [r_const], lambda e: e.tensor_copy(out=identb, in_=sm("ident", 0, 128)))
    G.op("act", [r_sm], [r_const], lambda e: e.activation(out=condf, in_=sm("c", 0, 16), func=AF.Silu))

    kz = [0]
    ky3 = [0]
    y3 = [(yb[0], r_yb[0]), (yb[1], r_yb[1]), (mbk, r_mb)]
    ky = [0]
    kt = [0]

    def modv(idx, what):
        base = idx * 96
        o = {"shift": 0, "scale": 16, "gate": 32, "A": 48, "HG": 64, "BG": 80}[what]
        return mods[:, base + o:base + o + 16]

    def ada_gen(l, s, eng, part="all"):
        idx = l * 3 + s
        base = idx * 96
        cbks = {"all": range(6), "ss": range(4), "gate": range(4, 6)}[part]
        for cbk in cbks:
            for kq in range(4):
                pc, rp = acquire(("ada", l, s, cbk, kq), owner="ada")
                for kk in range(4):
                    kc = 4 * kq + kk
                    if kc == 0:
                        G.op(eng, [rp, r_const], [r_acc],
                             lambda e, pc=pc: e.tensor_scalar(out=accA, in0=pc[:, 0:1024], scalar1=condf[:, 0:1],
                                                              scalar2=None, op0=ALU.mult))
                    else:
                        G.op(eng, [rp, r_const, r_acc], [r_acc],
                             lambda e, pc=pc, kk=kk, kc=kc: e.scalar_tensor_tensor(
                                 out=accA, in0=pc[:, kk * 1024:(kk + 1) * 1024], scalar=condf[:, kc:kc + 1], in1=accA,
                                 op0=ALU.mult, op1=ALU.add))
                    if kk == 3:
                        release(1, owner="ada")
                    yield False
            yi = ky[0] % 2
            ky[0] += 1

            def f2(e, yi=yi):
                ins = None
                for jj in range(8):
                    ins = e.matmul(yb[yi][:, jj:jj + 1], lhsT=accA[:, jj * 128:(jj + 1) * 128], rhs=onef[:, 0:1],
                                   start=True, stop=True)
                return ins
            G.op("pe", [r_acc, r_const], [r_yb[yi]], f2)
            G.op("dve", [r_yb[yi], r_sm], [r_mods[idx]],
                 lambda e, yi=yi, cbk=cbk: e.tensor_tensor(out=mods[:, base + cbk * 8:base + cbk * 8 + 8], in0=yb[yi][:, 0:8],
                                                           in1=sm("adab", idx * 48 + cbk * 8, idx * 48 + cbk * 8 + 8), op=ALU.add))
        if part in ("all", "ss"):
            G.op("dve", [r_mods[idx], r_sm], [r_mods[idx]],
                 lambda e: e.scalar_tensor_tensor(out=modv(idx, "A"), in0=modv(idx, "scale"), scalar=1.0,
                                                  in1=sm("ng", idx * 16, idx * 16 + 16), op0=ALU.add, op1=ALU.mult))
        if part == "ss":
            yield True
            return
        gs = 1.0 if s == 1 else 0.5
        G.op("dve", [r_mods[idx]], [r_mods[idx]],
             lambda e: e.tensor_scalar(out=modv(idx, "HG"), in0=modv(idx, "gate"), scalar1=gs, scalar2=None,
                                       op0=ALU.mult))
        if (l, s) == (1, 1):
            G.op("dve", [r_mods[idx], r_sm], [r_mods[idx]],
                 lambda e: e.tensor_tensor(out=modv(idx, "BG"), in0=modv(idx, "gate"),
                                           in1=sm("cvbout", 0, 16), op=ALU.mult))
        yield True

    adajob = {"gen": None, "tick": 0}

    def ada_start(jobs, eng="dve"):
        def chain():
            for job in jobs:
                l_, s_ = job[0], job[1]
                part = job[2] if len(job) > 2 else "all"
                for _ in ada_gen(l_, s_, eng, part):
                    yield False
                if part == "gate":
                    adajob["gate_done"] = True
            yield True
        adajob["gen"] = chain()
        adajob["tick"] = 0
        adajob["steps"] = ADA_STEPS * max(1, len(jobs))
        adajob["njobs"] = min(2, max(1, len(jobs)))

    def ada_drain():
        g = adajob["gen"]
        if g is not None:
            adajob["gen"] = None
            for _ in g:
                pass

    def ada_fine():
        g = adajob["gen"]
        if g is None:
            return
        adajob["gen"] = None
        done = next(g)
        adajob["gen"] = None if done else g

    def mrelease(k=1):
        release(k)
        g = adajob["gen"]
        if g is None or adajob.get("fine"):
            return
        adajob["tick"] += 1
        adajob["gen"] = None
        done = False
        for _ in range(adajob["steps"]):
            done = next(g)
            if done:
                break
        adajob["gen"] = None if done else g

    def norm_tile(idx, t0, n, hbuf, hoff, final=False):
        gr = gran(t0, n)
        hgr = gran(hoff, n)
        for kc in range(KC):
            G.op("act", [r_x[kc][g] for g in gr], [r_h[kc][g] for g in hgr],
                 lambda e, kc=kc: e.activation(out=hbuf[:, kc, hoff:hoff + n], in_=xs[:, kc, t0:t0 + n],
                                               func=AF.Square))

            def f(e, kc=kc):
                return e.matmul(sbk[:, 0:n], lhsT=onesD, rhs=hbuf[:, kc, hoff:hoff + n],
                                start=(kc == 0), stop=(kc == KC - 1))
            G.op("pe", [r_h[kc][g] for g in hgr] + [r_const], [r_sb], f)
        G.op("act", [r_sb], [r_rstd[0]],
             lambda e: e.activation(out=rstd[:, 0:n], in_=sbk[:, 0:n], func=AF.Sqrt, bias=epsc[:, 0:1], scale=1.0))
        G.op("dve", [r_rstd[0]], [r_rstd[0]],
             lambda e: e.reciprocal(out=rstd[:, 0:n], in_=rstd[:, 0:n]))
        for kc in range(KC):
            if final:
                G.op("dve", [r_x[kc][g] for g in gr] + [r_rstd[0]] + [r_sm],
                     [r_x[kc][g] for g in gr],
                     lambda e, kc=kc: e.scalar_tensor_tensor(out=xs[:, kc, t0:t0 + n], in0=xs[:, kc, t0:t0 + n],
                                                             scalar=sm("fg", kc, kc + 1), in1=rstd[:, 0:n],
                                                             op0=ALU.mult, op1=ALU.mult))
                continue
            k = kt[0] % 2
            kt[0] += 1
            G.op("dve", [r_x[kc][g] for g in gr] + [r_rstd[0]] + [r_mods[idx]], [r_tmpA[k]],
                 lambda e, kc=kc, k=k: e.scalar_tensor_tensor(out=tmpA[k][:, 0:n], in0=xs[:, kc, t0:t0 + n],
                                                              scalar=modv(idx, "A")[:, kc:kc + 1],
                                                              in1=rstd[:, 0:n], op0=ALU.mult, op1=ALU.mult))
            G.op("act", [r_tmpA[k], r_mods[idx]], [r_h[kc][g] for g in hgr],
                 lambda e, kc=kc, k=k: e.activation(out=hbuf[:, kc, hoff:hoff + n], in_=tmpA[k][:, 0:n],
                                                    func=AF.Identity, bias=modv(idx, "shift")[:, kc:kc + 1],
                                                    scale=1.0))

    def x_update(idx, dc, t0, n, ybank, r_ybank, extra_bias=False):
        gr = gran(t0, n)
        G.op("dve", [r_ybank, r_mods[idx]] + [r_x[dc][g] for g in gr], [r_x[dc][g] for g in gr],
             lambda e: e.scalar_tensor_tensor(out=xs[:, dc, t0:t0 + n], in0=ybank[:, 0:n],
                                              scalar=modv(idx, "HG")[:, dc:dc + 1], in1=xs[:, dc, t0:t0 + n],
                                              op0=ALU.mult, op1=ALU.add))
        if extra_bias:
            G.op("dve", [r_mods[idx]] + [r_x[dc][g] for g in gr], [r_x[dc][g] for g in gr],
                 lambda e: e.tensor_scalar(out=xs[:, dc, t0:t0 + n], in0=xs[:, dc, t0:t0 + n],
                                           scalar1=modv(idx, "BG")[:, dc:dc + 1], scalar2=None, op0=ALU.add))

    def loc_barrier():
        evs = [(e, G.count[e]) for e in ("pe", "act", "dve") if G.count[e] > 0]
        for e in ("pe", "act", "dve"):
            G.pending[e] = tuple(G.pending.get(e, ())) + tuple(evs)
        return evs

    def ffn(l, s, tiles, nxt):
        idx = l * 3 + s
        fine = [0]
        adajob["fine"] = True
        f = s // 2
        for (t0, n) in tiles:
            norm_tile(idx, t0, n, hF, t0)
        all_h = lambda t0, n: [r_h[kc][g] for kc in range(KC) for g in gran(t0, n)]

        def in_group(g):
            ab = g % 2
            for j in range(4):
                fc = 4 * g + j
                pc, rp = acquire(("fin", l, f, fc))
                for ti, (t0, n) in enumerate(tiles):
                    za, zu = kz[0] % 2 * 2, kz[0] % 2 * 2 + 1
                    kz[0] += 1
                    for half, zi in ((0, za), (1, zu)):
                        def fm(e, pc=pc, half=half, zi=zi, t0=t0, n=n):
                            ins = None
                            for kc in range(KC):
                                ins = e.matmul(zb[zi][:, 0:n],
                                               lhsT=pc[:, kc * 256 + half * 128:kc * 256 + half * 128 + 128],
                                               rhs=hF[:, kc, t0:t0 + n], start=(kc == 0), stop=(kc == KC - 1))
                            return ins
                        G.op("pe", [rp] + all_h(t0, n), [r_zb[zi]], fm)
                    k = kt[0] % 2
                    kt[0] += 1
                    G.op("act", [r_zb[za]], [r_sgt[k]],
                         lambda e, za=za, k=k, n=n: e.activation(out=sgt[k][:, 0:n], in_=zb[za][:, 0:n], func=AF.Silu))
                    G.op("dve", [r_sgt[k], r_zb[zu]], [r_act[ab][j][ti]],
                         lambda e, zu=zu, k=k, n=n, t0=t0, ab=ab, j=j: e.tensor_tensor(
                             out=actb[ab][:, j, t0:t0 + n], in0=sgt[k][:, 0:n], in1=zb[zu][:, 0:n], op=ALU.mult))
                    for _ in range(adajob.get("njobs", 1)):
                        ada_fine()
                mrelease()

        def out_group(g):
            ab = g % 2
            for hf in range(2):
                pc, rp = acquire(("fout", l, f, g, hf))
                for dcl in range(8):
                    dc = hf * 8 + dcl
                    for ti, (t0, n) in enumerate(tiles):
                        ybk, r_ybk = y3[ky3[0] % 3]
                        ky3[0] += 1

                        def fm(e, pc=pc, dcl=dcl, ybk=ybk, t0=t0, n=n, ab=ab):
                            ins = None
                            for j in range(4):
                                ins = e.matmul(ybk[:, 0:n], lhsT=pc[:, j * 1024 + dcl * 128:j * 1024 + dcl * 128 + 128],
                                               rhs=actb[ab][:, j, t0:t0 + n], start=(j == 0), stop=(j == 3))
                            return ins
                        G.op("pe", [rp] + [r_act[ab][j][ti] for j in range(4)], [r_ybk], fm)
                        x_update(idx, dc, t0, n, ybk, r_ybk)
                mrelease()

        for g in range(11):
            in_group(g)
            if g >= 1:
                while not adajob.get("gate_done", True):
                    ada_fine()
                out_group(g - 1)
        out_group(10)
        adajob["fine"] = False

    def gmlp(l, tiles, nxt, bev):
        idx = l * 3 + 1
        r_g = R("gm_const")
        r_vt = [R("vt") for _ in range(2)]
        r_vn = [R("vn") for _ in range(2)]
        r_junk = R("junk")
        r_G = [[R("G") for _ in range(5)] for _ in range(KC)]
        r_ut = [R("ut") for _ in range(2)]
        r_st = R("gstat")
        G.dma("sp", "ld2", [], [r_g], lambda e: e.dma_start(out=gtmp, in_=gc_d), extra=bev)
        for hd in range(16):
            G.op("dve", [r_g], [r_g],
                 lambda e, hd=hd: e.tensor_tensor(out=wsTm[:, hd, :], in0=gtmp[:, hd * 128:(hd + 1) * 128],
                                                  in1=gtmp[:, 2048:2176], op=ALU.mult))
        for q in range(4):
            def fr(e, q=q):
                return e.matmul(mbk[:, :], lhsT=ones1, rhs=wsTm[:, 4 * q:4 * q + 4, :].rearrange("p h t -> p (h t)"),
                                start=True, stop=True)
            G.op("pe", [r_g, r_const], [r_mb], fr)
            for hh in range(4):
                hd = 4 * q + hh
                G.op("dve", [r_mb, r_g, r_sm], [r_g],
                     lambda e, hd=hd, hh=hh: e.scalar_tensor_tensor(
                         out=Qc[:, hd, :], in0=mbk[:, hh * 128:(hh + 1) * 128], scalar=sm("gmlnb", hd, hd + 1),
                         in1=gtmp[:, 2176 + hd * 128:2176 + (hd + 1) * 128], op0=ALU.mult, op1=ALU.add))
        for r in r_vt + r_vn + [r_junk]:
            r.r.append(("dve", G.count["dve"]))
            r.r.append(("pe", G.count["pe"]))

        def gm_group(grp):
            base = grp[0][0]
            offs = [(t0, n, t0 - base) for (t0, n) in grp]
            ng = sum(n for (_, n) in grp)
            nch = ng // 128
            for (t0, n, off) in offs:
                norm_tile(idx, t0, n, hG, off)
            hres = lambda c: [r_h[kc][c] for kc in range(KC)]
            c = 0
            while c < nch:
                cs = [cc for cc in (c, c + 1) if cc < nch]
                for nt in range(8):
                    pc, rp = acquire(("gv2", nt))
                    for ci, cc in enumerate(cs):
                        zi = kz[0] % 4
                        kz[0] += 1

                        def fm(e, pc=pc, cc=cc, zi=zi):
                            ins = None
                            for kc in range(KC):
                                ins = e.matmul(zb[zi][:, 0:256], lhsT=hG[:, kc, cc * 128:(cc + 1) * 128],
                                               rhs=pc[:, kc * 256:(kc + 1) * 256],
                                               start=(kc == 0), stop=(kc == KC - 1))
                            return ins
                        G.op("pe", [rp] + hres(cc), [r_zb[zi]], fm)
                        G.op("act", [r_zb[zi]], [r_vt[ci]],
                             lambda e, zi=zi, ci=ci, nt=nt: e.activation(out=vt[ci][:, nt * 256:(nt + 1) * 256],
                                                                         in_=zb[zi][:, 0:256], func=AF.Gelu))
                    mrelease()
                for ci, cc in enumerate(cs):
                    st = smstat[:, ci * 8:ci * 8 + 8]
                    G.op("dve", [r_vt[ci]], [r_st],
                         lambda e, ci=ci, st=st: e.tensor_reduce(out=st[:, 0:1], in_=vt[ci], axis=AX.X, op=ALU.add))
                    G.op("dve", [r_st], [r_st],
                         lambda e, st=st: e.tensor_scalar(out=st[:, 1:2], in0=st[:, 0:1], scalar1=-1.0 / D,
                                                          scalar2=None, op0=ALU.mult))
                    G.op("dve", [r_st], [r_st], lambda e, st=st: e.memset(st[:, 2:3], 0.0))
                    G.op("act", [r_vt[ci], r_st], [r_junk, r_st],
                         lambda e, ci=ci, st=st: e.activation(out=junk, in_=vt[ci], func=AF.Square, bias=st[:, 1:2],
                                                              scale=1.0, accum_out=st[:, 2:3]))
                    G.op("act", [r_st], [r_st],
                         lambda e, st=st: e.activation(out=st[:, 3:4], in_=st[:, 2:3], func=AF.Sqrt, bias=epsc[:, 0:1],
                                                       scale=1.0 / D))
                    G.op("dve", [r_st], [r_st],
                         lambda e, st=st: e.reciprocal(out=st[:, 3:4], in_=st[:, 3:4]))
                    G.op("dve", [r_st], [r_st],
                         lambda e, st=st: e.tensor_tensor(out=st[:, 4:5], in0=st[:, 1:2], in1=st[:, 3:4], op=ALU.mult))
                    G.op("act", [r_vt[ci], r_st], [r_vn[ci]],
                         lambda e, ci=ci, st=st: e.activation(out=vn[ci], in_=vt[ci], func=AF.Identity,
                                                              bias=st[:, 4:5], scale=st[:, 3:4]))
                    for q in range(4):
                        bk, rbk = (mbk, r_mb) if q % 2 == 0 else (sbk, r_sb)

                        def fs(e, ci=ci, q=q, bk=bk):
                            ins = None
                            for hh in range(4):
                                hd = 4 * q + hh
                                ins = e.matmul(bk[:, hh * 128:(hh + 1) * 128], lhsT=vn[ci][:, hd * 128:(hd + 1) * 128],
                                               rhs=wsTm[:, hd, :], start=True, stop=True)
                            return ins
                        G.op("pe", [r_vn[ci], r_g], [rbk], fs)
                        for hh in range(4):
                            hd = 4 * q + hh
                            G.op("dve", [rbk, r_g, r_sm], [r_G[hd][cc]],
                                 lambda e, hd=hd, hh=hh, cc=cc, bk=bk: e.scalar_tensor_tensor(
                                     out=Gs[:, hd, cc * 128:(cc + 1) * 128], in0=bk[:, hh * 128:(hh + 1) * 128],
                                     scalar=sm("gmlng", hd, hd + 1), in1=Qc[:, hd, :], op0=ALU.mult, op1=ALU.add))
                c += 2
            for ep in range(8):
                pc, rp = acquire(("gu", ep))
                for e2 in range(2):
                    ec = 2 * ep + e2
                    for (t0, n, off) in offs:
                        zi = kz[0] % 4
                        kz[0] += 1
                        hg = gran(off, n)

                        def fm(e, pc=pc, e2=e2, zi=zi, n=n, off=off):
                            ins = None
                            for kc in range(KC):
                                ins = e.matmul(zb[zi][:, 0:n], lhsT=pc[:, e2 * 2048 + kc * 128:e2 * 2048 + (kc + 1) * 128],
                                               rhs=hG[:, kc, off:off + n], start=(kc == 0), stop=(kc == KC - 1))
                            return ins
                        G.op("pe", [rp] + [r_h[kc][g] for kc in range(KC) for g in hg], [r_zb[zi]], fm)
                        k = kt[0] % 2
                        kt[0] += 1
                        G.op("act", [r_zb[zi]], [r_ut[k]],
                             lambda e, zi=zi, k=k, n=n: e.activation(out=utmp[k][:, 0:n], in_=zb[zi][:, 0:n], func=AF.Gelu))
                        G.op("dve", [r_ut[k]] + [r_G[ec][g] for g in hg], [r_G[ec][g] for g in hg],
                             lambda e, ec=ec, k=k, n=n, off=off: e.tensor_tensor(out=Gs[:, ec, off:off + n], in0=utmp[k][:, 0:n],
                                                                                 in1=Gs[:, ec, off:off + n], op=ALU.mult))
                mrelease()
            for dp in range(8):
                pc, rp = acquire(("go", dp))
                for d2 in range(2):
                    dc = 2 * dp + d2
                    for (t0, n, off) in offs:
                        yi = ky[0] % 2
                        ky[0] += 1
                        hg = gran(off, n)

                        def fm(e, pc=pc, d2=d2, yi=yi, n=n, off=off):
                            ins = None
                            for ec in range(KC):
                                ins = e.matmul(yb[yi][:, 0:n], lhsT=pc[:, ec * 256 + d2 * 128:ec * 256 + d2 * 128 + 128],
                                               rhs=Gs[:, ec, off:off + n], start=(ec == 0), stop=(ec == KC - 1))
                            return ins
                        G.op("pe", [rp] + [r_G[ec][g] for ec in range(KC) for g in hg], [r_yb[yi]], fm)
                        x_update(idx, dc, t0, n, yb[yi], r_yb[yi])
                mrelease()

        gm_group(tiles[0:2])
        gm_group(tiles[2:3])

    def conv(l, tiles, nxt):
        idx = l * 3 + 1
        r_c2 = [R("c2") for _ in range(KC)]
        r_yb2 = [R("ybuf") for _ in range(2)]
        r_yt = [R("ytail") for _ in range(KC)]
        r_stmp = [R("stmp") for _ in range(2)]
        r_dg = [R("dg") for _ in range(2)]
        r_cbq = [R("cbq") for _ in range(2)]
        r_s1s = R("s1s")
        r_rc = R("rstdc")

        def cv_group(grp):
            base = grp[0][0]
            halo = len(grp) == 2
            for (t0_, n_) in grp:
                norm_tile(idx, t0_, n_, hC, t0_ - base)
            t0, n = grp[-1]
            offm = t0 - base
            gr = gran(offm, n)

            def inproj(pc, rp, off_, n_):
                za, zg = kz[0] % 2 * 2, kz[0] % 2 * 2 + 1
                kz[0] += 1
                hg = gran(off_, n_)
                for half, zi in ((0, za), (1, zg)):
                    def fm(e, pc=pc, half=half, zi=zi, off_=off_, n_=n_):
                        ins = None
                        for kc in range(KC):
                            ins = e.matmul(zb[zi][:, 0:n_], lhsT=pc[:, kc * 256 + half * 128:kc * 256 + half * 128 + 128],
                                           rhs=hC[:, kc, off_:off_ + n_], start=(kc == 0), stop=(kc == KC - 1))
                        return ins
                    G.op("pe", [rp] + [r_h[kc][g] for kc in range(KC) for g in hg], [r_zb[zi]], fm)
                return za, zg

            def glu(cc, yk, za, zg, n_):
                k = kt[0] % 2
                kt[0] += 1
                G.op("act", [r_zb[zg], r_sm], [r_stmp[k]],
                     lambda e, zg=zg, k=k, cc=cc, n_=n_: e.activation(out=stmp[k][:, 0:n_], in_=zb[zg][:, 0:n_], func=AF.Sigmoid,
                                                                      bias=sm("cvbin", 16 + cc, 17 + cc), scale=1.0))
                G.op("dve", [r_zb[za], r_stmp[k], r_sm], [r_yb2[yk]],
                     lambda e, za=za, k=k, cc=cc, yk=yk, n_=n_: e.scalar_tensor_tensor(
                         out=ybuf[yk][:, 30:30 + n_], in0=zb[za][:, 0:n_], scalar=sm("cvbin", cc, cc + 1),
                         in1=stmp[k][:, 0:n_], op0=ALU.add, op1=ALU.mult))

            def conv_mm(cc, stats_cc=None):
                yk = cc % 2
                yi = ky[0] % 2
                ky[0] += 1

                def fc(e, yk=yk, yi=yi):
                    ins = None
                    for j in range(31):
                        ins = e.matmul(yb[yi][:, 0:n], lhsT=dg[yk][:, j, :], rhs=ybuf[yk][:, j:j + n],
                                       start=(j == 0), stop=(j == 30))
                    return ins
                G.op("pe", [r_dg[yk], r_yb2[yk]], [r_yb[yi]], fc)
                if stats_cc is not None:
                    stats_mm(stats_cc)
                G.op("act", [r_yb[yi], r_sm], [r_c2[cc]],
                     lambda e, yi=yi, cc=cc: e.activation(out=c2[:, cc, 0:n], in_=yb[yi][:, 0:n], func=AF.Identity,
                                                          bias=sm("dwb", cc, cc + 1), scale=1.0))
                G.op("act", [r_yb[yi], r_sm], [r_cbq[0]],
                     lambda e, yi=yi, cc=cc: e.activation(out=cbq[0][:, 0:n], in_=yb[yi][:, 0:n], func=AF.Identity,
                                                          bias=sm("dwb", cc, cc + 1), scale=1.0))
                G.op("act", [r_yb[yi], r_sm], [r_cbq[1]],
                     lambda e, yi=yi, cc=cc: e.activation(out=cbq[1][:, 0:n], in_=yb[yi][:, 0:n], func=AF.Square,
                                                          bias=sm("dwb", cc, cc + 1), scale=1.0))

            def stats_mm(cc):
                G.op("pe", [r_cbq[0], r_const], [r_sb],
                     lambda e, cc=cc: e.matmul(sbk[:, 0:n], lhsT=onesD, rhs=cbq[0][:, 0:n], start=(cc == 0), stop=(cc == KC - 1)))
                G.op("pe", [r_cbq[1], r_const], [r_mb],
                     lambda e, cc=cc: e.matmul(mbk[:, 0:n], lhsT=onesD, rhs=cbq[1][:, 0:n], start=(cc == 0), stop=(cc == KC - 1)))

            for cc in range(KC):
                yk = cc % 2
                G.op("dve", [r_sm, r_const], [r_dg[yk]],
                     lambda e, cc=cc, yk=yk: e.tensor_tensor(
                         out=dg[yk], in0=identb.unsqueeze(1).to_broadcast([128, 31, 128]),
                         in1=sm("dw", cc * 31, cc * 31 + 31).unsqueeze(2).to_broadcast([128, 31, 128]), op=ALU.mult))
                pc, rp = acquire(("ci", cc))
                if halo:
                    za, zg = inproj(pc, rp, 0, 128)
                    glu(cc, yk, za, zg, 128)
                    G.op("dve", [r_yb2[yk], r_sm], [r_yt[cc]],
                         lambda e, cc=cc, yk=yk: e.tensor_scalar(out=ytail[:, cc, :], in0=ybuf[yk][:, 128:158],
                                                                 scalar1=sm("hmask", 0, 1), scalar2=None, op0=ALU.mult))
                za, zg = inproj(pc, rp, offm, n)
                mrelease()
                glu(cc, yk, za, zg, n)
                G.op("dve", [r_yt[cc]], [r_yb2[yk]],
                     lambda e, cc=cc, yk=yk: e.tensor_copy(out=ybuf[yk][:, 0:30], in_=ytail[:, cc, :]))
                G.op("dve", [r_yb2[yk]], [r_yt[cc]],
                     lambda e, cc=cc, yk=yk: e.tensor_copy(out=ytail[:, cc, :], in_=ybuf[yk][:, n:n + 30]))
                if cc >= 1:
                    conv_mm(cc - 1, cc - 2 if cc >= 2 else None)
            conv_mm(KC - 1, KC - 2)
            stats_mm(KC - 1)
            G.op("dve", [r_sb], [r_s1s], lambda e: e.tensor_copy(out=s1s[:, 0:n], in_=sbk[:, 0:n]))
            G.op("dve", [r_s1s], [r_rc],
                 lambda e: e.tensor_tensor(out=rstdc[:, 0:n], in0=s1s[:, 0:n], in1=s1s[:, 0:n], op=ALU.mult))
            G.op("dve", [r_mb, r_rc], [r_rc],
                 lambda e: e.tensor_tensor(out=rstdc[:, 0:n], in0=mbk[:, 0:n], in1=rstdc[:, 0:n], op=ALU.subtract))
            G.op("act", [r_rc], [r_rc],
                 lambda e: e.activation(out=rstdc[:, 0:n], in_=rstdc[:, 0:n], func=AF.Sqrt, bias=epsc[:, 0:1], scale=1.0))
            G.op("dve", [r_rc], [r_rc],
                 lambda e: e.reciprocal(out=rstdc[:, 0:n], in_=rstdc[:, 0:n]))
            for cc in range(KC):
                G.op("dve", [r_c2[cc], r_s1s], [r_c2[cc]],
                     lambda e, cc=cc: e.tensor_tensor(out=c2[:, cc, 0:n], in0=c2[:, cc, 0:n], in1=s1s[:, 0:n], op=ALU.subtract))
                G.op("dve", [r_c2[cc], r_rc], [r_c2[cc]],
                     lambda e, cc=cc: e.tensor_tensor(out=c2[:, cc, 0:n], in0=c2[:, cc, 0:n], in1=rstdc[:, 0:n], op=ALU.mult))
                G.op("act", [r_c2[cc], r_sm], [r_h[cc][g] for g in gr],
                     lambda e, cc=cc: e.activation(out=cact[:, cc, offm:offm + n], in_=c2[:, cc, 0:n], func=AF.Silu,
                                                   scale=sm("cvlng", cc, cc + 1), bias=sm("cvlnb", cc, cc + 1)))
            allh = [r_h[kc][g] for kc in range(KC) for g in gr]
            for dp in range(8):
                pc, rp = acquire(("co", dp))
                for d2 in range(2):
                    dc = 2 * dp + d2
                    yi = ky[0] % 2
                    ky[0] += 1

                    def fm(e, pc=pc, d2=d2, yi=yi):
                        ins = None
                        for cc in range(KC):
                            ins = e.matmul(yb[yi][:, 0:n], lhsT=pc[:, cc * 256 + d2 * 128:cc * 256 + d2 * 128 + 128],
                                           rhs=cact[:, cc, offm:offm + n], start=(cc == 0), stop=(cc == KC - 1))
                        return ins
                    G.op("pe", [rp] + allh, [r_yb[yi]], fm)
                    x_update(idx, dc, t0, n, yb[yi], r_yb[yi], extra_bias=True)
                mrelease()

        cv_group(tiles[0:2])
        cv_group(tiles[2:3])

    def upcoming(i):
        jobs = []
        j = i + 1
        while j < len(subs):
            jobs.append(subs[j])
            if subs[j][1] != 1:
                break
            j += 1
        return jobs

    defer_gate = subs[0][1] != 1
    pro = [subs[0] + ("ss",)] if defer_gate else [subs[0]] + upcoming(0)
    ada_start(pro)
    ada_drain()
    tiles = list(TILES)
    for i, (l, s) in enumerate(subs):
        nxt = subs[i + 1] if i + 1 < len(subs) else None
        bev = loc_barrier()
        jobs = ([subs[0] + ("gate",)] if (i == 0 and defer_gate) else []) + (upcoming(i) if s != 1 else [])
        if i == 0 and defer_gate:
            adajob["gate_done"] = False
        if jobs:
            ada_start(jobs)
        if s != 1:
            ffn(l, s, tiles, nxt)
        elif l == 0:
            gmlp(l, tiles, nxt, bev)
        else:
            conv(l, tiles, nxt)
            tiles = TILES[1:]
        ada_drain()
    if plan is not None:
        assert ringst["acq"] == NP, (ringst["acq"], NP)
    loc_barrier()
    outv = out_d.rearrange("(c p) t -> p c t", p=128)
    for (t0, n) in TILES[1:]:
        if do_final:
            norm_tile(0, t0, n, hF, t0, final=True)
        for q in range(4):
            rd = [r_x[kc][g] for kc in range(4 * q, 4 * q + 4) for g in gran(t0, n)]
            G.dma("sp", "st", rd, [],
                  lambda e, q=q, t0=t0, n=n: e.dma_start(out=outv[:, 4 * q:4 * q + 4, t0 - 128:t0 - 128 + n],
                                                         in_=xs[:, 4 * q:4 * q + 4, t0:t0 + n]))
    G.final_wait("sp", [("st", G.dma_count["st"])])

    semnames = list(Gen.ENG) + sorted(G.dma_count.keys())
    sems = {}
    from contextlib import ExitStack
    with ExitStack() as es:
        for nme in semnames:
            sems[nme] = es.enter_context(nc.semaphore("s_" + nme))
        block = es.enter_context(nc.Block())

        waited_vals = {e_: set() for e_ in Gen.ENG}
        for e_ in Gen.ENG:
            for waits, fn, inc in G.ops[e_]:
                for (k, v) in waits:
                    if k in waited_vals:
                        waited_vals[k].add(v)
        rank = {e_: {v: i + 1 for i, v in enumerate(sorted(waited_vals[e_]))} for e_ in Gen.ENG}

        def emit(engname):
            def body(e):
                for waits, fn, inc in G.ops[engname]:
                    for (k, v) in waits:
                        e.wait_ge(sems[k], rank[k][v] if k in rank else v)
                    if fn is None:
                        continue
                    ins = fn(e)
                    if inc[0] in rank:
                        if inc[1] in rank[inc[0]]:
                            ins.then_inc(sems[inc[0]], 1)
                    else:
                        ins.then_inc(sems[inc[0]], inc[1])
            return body
        block.tensor(emit("pe"))
        block.scalar(emit("act"))
        block.vector(emit("dve"))
        block.gpsimd(emit("pool"))
        block.sync(emit("sp"))
    return nc, rec


def _pp(v, ncol):
    return np.ascontiguousarray(np.asarray(v, np.float32).reshape(ncol, 128).T)


def make_in_maps(inp, specs):
    ws = np.empty((len(specs), 128, PIECE), np.float32)
    for i, sp_ in enumerate(specs):
        ws[i] = pack_piece(sp_, inp)
    gconst = np.empty((128, NGC), np.float32)
    gconst[:, 0:2048] = np.transpose(inp["gm_ws"][0], (2, 0, 1)).reshape(128, 2048)
    gconst[:, 2048:2176] = np.triu(np.ones((128, 128), np.float32))
    gconst[:, 2176:] = np.broadcast_to(inp["gm_bs"][0].reshape(1, 2048), (128, 2048))
    maps = []
    for k in range(NCORE):
        b, q = divmod(k, 4)
        sm_ = np.zeros((128, NSM), np.float32)

        def put(name, arr):
            sm_[:, SM[name]:SM[name] + arr.shape[1]] = arr
        put("c", _pp(inp["c"][b], 16))
        put("adab", np.concatenate([_pp(inp["ada_b"][l], 144) for l in range(2)], axis=1))
        put("ng", np.concatenate([_pp(inp["norm_g"][l, s], 16) for l in range(2) for s in range(3)], axis=1))
        put("fg", _pp(inp["final_g"], 16))
        put("cvbin", _pp(inp["cv_b_in"][0], 32))
        dw = inp["cv_dw_w"][0]
        put("dw", np.ascontiguousarray(dw.reshape(31, 16, 128).transpose(2, 1, 0)).reshape(128, 496))
        put("dwb", _pp(inp["cv_dw_b"][0], 16))
        put("cvlng", _pp(inp["cv_ln_g"][0], 16))
        put("cvlnb", _pp(inp["cv_ln_b"][0], 16))
        put("cvbout", _pp(inp["cv_b_out"][0], 16))
        put("gmlng", _pp(inp["gm_ln_g"][0], 16))
        put("gmlnb", _pp(inp["gm_ln_b"][0], 16))
        sm_[:, SM["hmask"]] = 0.0 if q == 0 else 1.0
        sm_[:, SM["ident"]:SM["ident"] + 128] = np.eye(128, dtype=np.float32)
        xT = np.zeros((D, T), np.float32)
        s0 = q * TM
        xT[:, 128:] = inp["x"][b, s0:s0 + TM, :].T
        if q > 0:
            xT[:, :128] = inp["x"][b, s0 - 128:s0, :].T
        maps.append({"xT": xT, "smalls": sm_, "gconst": gconst, "wstream": ws})
    return maps


_CACHE = {}


def run(inp, subs=None, do_final=True, cores=NCORE, trace=False):
    subs = ALL_SUBS if subs is None else subs
    key = (tuple(subs), do_final)
    if key not in _CACHE:
        _CACHE[key] = build_program(subs, do_final)
    nc, specs = _CACHE[key]
    maps = make_in_maps(inp, specs)[:cores]
    res = run_bass_kernel_spmd(nc, maps, core_ids=list(range(cores)), trace=trace)
    outs = [np.ascontiguousarray(r["outT"].T) for r in res.results]
    return outs, res


def kernel(**inputs):
    inp = {k: np.asarray(v) for k, v in inputs.items()}
    outs, _ = run(inp)
    out = np.empty((2, 4096, D), np.float32)
    for k in range(NCORE):
        b, q = divmod(k, 4)
        out[b, q * TM:(q + 1) * TM, :] = outs[k]
    return out
```
